# Optimizing a Trainium2 kernel written in Bass

```python
import math
import jax, jax.numpy as jnp
from jax import lax
import numpy as np

D_MODEL = 1024
BATCH = 4
SEQ = 4096
DEPTH = 2

CHUNK = 64
Q_BLOCK = 128
HEAD_DIM = 64
N_HEADS_DIFF = 4
DIFF_V_DIM = 2 * HEAD_DIM
N_HEADS_SB = 8
N_HEADS_FOX = 8
W_DIFF = N_HEADS_DIFF * DIFF_V_DIM
W_SB = N_HEADS_SB * HEAD_DIM
W_FOX = N_HEADS_FOX * HEAD_DIM
D_MIX = W_DIFF + W_SB + W_FOX
ROT_DIM = HEAD_DIM // 4
ROPE_THETA = 500000.0
NORM_EPS = 1e-6
IN_SPLITS = [W_DIFF, W_DIFF, W_DIFF, W_DIFF,
             W_SB, W_SB, W_SB, W_SB,
             W_FOX, W_FOX, W_FOX, W_FOX,
             N_HEADS_FOX]
D_IN = sum(IN_SPLITS)

kernel_name = "hybrid_diff_stickbreak_fox_block"


def rms_norm(x, gain):
    xf = x.astype(jnp.float32)
    y = xf * lax.rsqrt(jnp.mean(xf * xf, axis=-1, keepdims=True) + NORM_EPS)
    return (y * gain.astype(jnp.float32)).astype(x.dtype)


def rope_tables(seq):
    pos = jnp.arange(seq, dtype=jnp.float32)
    inv_freq = ROPE_THETA ** (-jnp.arange(0, ROT_DIM, 2, dtype=jnp.float32) / ROT_DIM)
    ang = pos[:, None] * inv_freq[None, :]
    return jnp.cos(ang), jnp.sin(ang)


def apply_partial_rope(t, cos, sin):
    rot, rest = t[..., :ROT_DIM], t[..., ROT_DIM:]
    r1, r2 = rot[..., :ROT_DIM // 2], rot[..., ROT_DIM // 2:]
    rot = jnp.concatenate([r1 * cos - r2 * sin, r2 * cos + r1 * sin], axis=-1).astype(t.dtype)
    return jnp.concatenate([rot, rest], axis=-1)


def diff_attention(q, k, v, lam, lambda_init, sub_gain):
    seq = q.shape[3]
    scale = q.shape[-1] ** -0.5
    chunk_id = jnp.arange(seq) // CHUNK
    outs = []
    for i in range(seq // Q_BLOCK):
        q0, q1 = i * Q_BLOCK, (i + 1) * Q_BLOCK
        s = jnp.einsum('bhcqd,bhckd->bhcqk', q[:, :, :, q0:q1], k[:, :, :, :q1]).astype(jnp.float32) * scale
        allowed = chunk_id[:q1][None, :] <= chunk_id[q0:q1][:, None]
        p = jax.nn.softmax(jnp.where(allowed, s, -jnp.inf), axis=-1)
        w = p[:, :, 0] - lam * p[:, :, 1]
        outs.append(jnp.einsum('bhqk,bhkd->bhqd', w, v[:, :, :q1].astype(jnp.float32)))
    o = jnp.concatenate(outs, axis=2)
    return rms_norm(o, sub_gain) * (1.0 - lambda_init)


def stick_breaking_attention(q, k, v):
    seq = q.shape[2]
    scale = q.shape[-1] ** -0.5
    pos = jnp.arange(seq)
    outs = []
    for i in range(seq // Q_BLOCK):
        q0, q1 = i * Q_BLOCK, (i + 1) * Q_BLOCK
        z = jnp.einsum('bhqd,bhkd->bhqk', q[:, :, q0:q1], k[:, :, :q1]).astype(jnp.float32) * scale
        strict = pos[:q1][None, :] < pos[q0:q1][:, None]
        log_keep = jnp.where(strict, jax.nn.log_sigmoid(-z), 0.0)
        log_rest = lax.cumsum(log_keep, axis=3, reverse=True) - log_keep
        a = jnp.where(strict, jnp.exp(jax.nn.log_sigmoid(z) + log_rest), 0.0)
        outs.append(jnp.einsum('bhqk,bhkd->bhqd', a, v[:, :, :q1].astype(jnp.float32)))
    return jnp.concatenate(outs, axis=2)


def forgetting_attention(q, k, v, cum_log_f):
    seq = q.shape[2]
    scale = q.shape[-1] ** -0.5
    pos = jnp.arange(seq)
    outs = []
    for i in range(seq // Q_BLOCK):
        q0, q1 = i * Q_BLOCK, (i + 1) * Q_BLOCK
        logits = jnp.einsum('bhqd,bhkd->bhqk', q[:, :, q0:q1], k[:, :, :q1]).astype(jnp.float32) * scale
        logits = logits + cum_log_f[:, :, q0:q1, None] - cum_log_f[:, :, None, :q1]
        causal = pos[:q1][None, :] <= pos[q0:q1][:, None]
        p = jax.nn.softmax(jnp.where(causal, logits, -jnp.inf), axis=-1)
        outs.append(jnp.einsum('bhqk,bhkd->bhqd', p, v[:, :, :q1].astype(jnp.float32)))
    return jnp.concatenate(outs, axis=2)


def hybrid_layer(x, w_in, f_bias, lam_vec, subln, w_out, g_pre, g_post, layer_idx, cos, sin):
    b, s, _ = x.shape
    dt = x.dtype
    h = rms_norm(x, g_pre)
    p = h @ w_in
    split_at = np.cumsum(IN_SPLITS)[:-1].tolist()
    qa, ka, va, ga, qs, ks, vs, gs, qf, kf, vf, gf, ff = jnp.split(p, split_at, axis=-1)

    def heads(t, n, d):
        return t.reshape(b, s, n, d).transpose(0, 2, 1, 3)

    def merge(t):
        return t.transpose(0, 2, 1, 3).reshape(b, s, -1).astype(dt)

    qa = apply_partial_rope(qa.reshape(b, s, N_HEADS_DIFF, 2, HEAD_DIM).transpose(0, 2, 3, 1, 4), cos, sin)
    ka = apply_partial_rope(ka.reshape(b, s, N_HEADS_DIFF, 2, HEAD_DIM).transpose(0, 2, 3, 1, 4), cos, sin)
    lambda_init = 0.8 - 0.6 * math.exp(-0.3 * layer_idx)
    lv = lam_vec.astype(jnp.float32)
    lam = jnp.exp(jnp.sum(lv[0] * lv[1])) - jnp.exp(jnp.sum(lv[2] * lv[3])) + lambda_init
    ya = merge(diff_attention(qa, ka, heads(va, N_HEADS_DIFF, DIFF_V_DIM), lam, lambda_init, subln))
    ya = ya * jax.nn.silu(ga)

    yb = merge(stick_breaking_attention(heads(qs, N_HEADS_SB, HEAD_DIM), heads(ks, N_HEADS_SB, HEAD_DIM),
                                        heads(vs, N_HEADS_SB, HEAD_DIM)))
    yb = yb * jax.nn.silu(gs)

    log_f = jax.nn.log_sigmoid((ff + f_bias).astype(jnp.float32)).transpose(0, 2, 1)
    cum_log_f = jnp.cumsum(log_f, axis=-1)
    yc = merge(forgetting_attention(heads(qf, N_HEADS_FOX, HEAD_DIM), heads(kf, N_HEADS_FOX, HEAD_DIM),
                                    heads(vf, N_HEADS_FOX, HEAD_DIM), cum_log_f))
    yc = yc * jax.nn.silu(gf)

    y = jnp.concatenate([ya, yb, yc], axis=-1) @ w_out
    return x + rms_norm(y, g_post)


def setup_inputs(seed: int = 0) -> dict:
    key = jax.random.key(seed)
    ks = jax.random.split(key, 8)
    x = jax.random.normal(ks[0], (BATCH, SEQ, D_MODEL), jnp.float32)
    w_in = jax.random.normal(ks[1], (DEPTH, D_MODEL, D_IN), jnp.float32) * D_MODEL ** -0.5
    forget_bias = jax.random.uniform(ks[2], (DEPTH, N_HEADS_FOX), jnp.float32, minval=1.0, maxval=4.0)
    diff_lambda = 0.1 * jax.random.normal(ks[3], (DEPTH, 4, HEAD_DIM), jnp.float32)
    diff_subln = 1.0 + 0.02 * jax.random.normal(ks[4], (DEPTH, DIFF_V_DIM), jnp.float32)
    w_out = jax.random.normal(ks[5], (DEPTH, D_MIX, D_MODEL), jnp.float32) * D_MIX ** -0.5
    pre_norm = 1.0 + 0.02 * jax.random.normal(ks[6], (DEPTH, D_MODEL), jnp.float32)
    post_norm = 1.0 + 0.02 * jax.random.normal(ks[7], (DEPTH, D_MODEL), jnp.float32)
    return {"x": x, "w_in": w_in, "forget_bias": forget_bias, "diff_lambda": diff_lambda,
            "diff_subln": diff_subln, "w_out": w_out, "pre_norm": pre_norm, "post_norm": post_norm}


def reference(x, w_in, forget_bias, diff_lambda, diff_subln, w_out, pre_norm, post_norm):
    cos, sin = rope_tables(x.shape[1])
    for l in range(DEPTH):
        x = hybrid_layer(x, w_in[l], forget_bias[l], diff_lambda[l], diff_subln[l], w_out[l],
                         pre_norm[l], post_norm[l], l, cos, sin)
    return x
```

```python
import contextlib
import math
import numpy as np
import ml_dtypes
import concourse.bass as bass
import concourse.mybir as mybir
from concourse.bass_utils import run_bass_kernel_spmd

F32 = mybir.dt.float32
BF16 = mybir.dt.bfloat16
AF = mybir.ActivationFunctionType
ALU = mybir.AluOpType

D_MODEL = 1024
DEPTH = 2
NEG = -30000.0
EPS = 1e-6
ROPE_THETA = 500000.0
KC = 8

CB_IDENT, CB_NEGTRI, CB_NEGONES, CB_MASKA, CB_MASKB, CB_MASKC, CB_ONES, NCB = 0, 128, 256, 384, 1280, 2176, 3072, 3200
CF_MEAN, CF_ONES, CF_SEL, CF_IDENT4, NCF = 0, 128, 256, 768, 772

COMPUTE = ("pe", "act", "dve", "pool")


class _Op:
    __slots__ = ("eng", "fn", "dma", "sig", "deps", "dprev", "cc")

    def __init__(self, eng, fn, dma, cc=False):
        self.eng = eng
        self.fn = fn
        self.dma = dma
        self.cc = cc
        self.sig = None
        self.deps = {}
        self.dprev = None


class Prog:
    def __init__(self, n_dma_sems=16):
        self.ops = []
        self.last_writer = {}
        self.readers = {}
        self.n_dma_sems = n_dma_sems

    def op(self, eng, fn, reads=(), writes=(), dma=False, cc=False):
        o = _Op(eng, fn, dma, cc)
        for k in reads:
            w = self.last_writer.get(k)
            if w is not None:
                o.deps[w] = "raw"
        for k in writes:
            w = self.last_writer.get(k)
            if w is not None and w not in o.deps:
                o.deps[w] = "waw"
            for r in self.readers.get(k, ()):
                if r not in o.deps:
                    o.deps[r] = "war"
        for k in reads:
            self.readers.setdefault(k, []).append(o)
        for k in writes:
            self.last_writer[k] = o
            self.readers[k] = []
        self.ops.append(o)
        return o

    def pe(self, fn, reads=(), writes=()):
        return self.op("pe", fn, reads, writes)

    def act(self, fn, reads=(), writes=()):
        return self.op("act", fn, reads, writes)

    def dve(self, fn, reads=(), writes=()):
        return self.op("dve", fn, reads, writes)

    def pool(self, fn, reads=(), writes=()):
        return self.op("pool", fn, reads, writes)

    def dma(self, q, fn, reads=(), writes=()):
        return self.op(q, fn, reads, writes, dma=True)

    def cc(self, fn, reads=(), writes=()):
        return self.op("pool", fn, reads, writes, dma=True, cc=True)

    def emit(self, nc, final_wait_engine="sp"):
        ops = self.ops
        need = {}
        signalled = set()
        for o in ops:
            lst = []
            for d, kind in o.deps.items():
                if d.dma or o.dma:
                    lst.append(d)
                elif d.eng == o.eng:
                    if o.eng == "pe":
                        continue
                    lst.append(d)
                else:
                    lst.append(d)
            need[o] = lst
            signalled.update(lst)
        cnt = {e: 0 for e in COMPUTE}
        dcnt, duse = {}, {}
        ncc = 0
        for o in ops:
            if o.cc:
                ncc += 1
                o.sig = (("cc",), ncc)
            elif o.dma:
                i = dcnt.get(o.eng, 0)
                dcnt[o.eng] = i + 1
                key = ("dma", o.eng, i % self.n_dma_sems)
                u = duse.get(key, 0) + 1
                duse[key] = u
                o.sig = (key, 16 * u)
                o.dprev = (key, 16 * (u - 1)) if u > 1 else None
            elif o in signalled:
                cnt[o.eng] += 1
                o.sig = (o.eng, cnt[o.eng])
        semkeys = sorted(set(o.sig[0] for o in ops if o.sig is not None), key=str)
        self.stats = dict(n_ops=len(ops), sig=dict(cnt), n_sems=len(semkeys))
        with contextlib.ExitStack() as es:
            sems = {}
            for k in semkeys:
                nm = "s_" + "_".join(str(x) for x in (k if isinstance(k, tuple) else (k,)))
                sems[k] = es.enter_context(nc.semaphore(nm))
            block = es.enter_context(nc.Block())
            dmas = [o for o in ops if o.dma]

            def run(engname, E):
                known = {}
                nwait = 0
                for o in ops:
                    if o.eng != engname:
                        continue
                    ws = {}
                    for d in need[o]:
                        k, v = d.sig
                        if ws.get(k, 0) < v:
                            ws[k] = v
                    if o.dma and o.dprev is not None:
                        k, v = o.dprev
                        if ws.get(k, 0) < v:
                            ws[k] = v
                    for k, v in ws.items():
                        if known.get(k, 0) >= v:
                            continue
                        E.wait_ge(sems[k], v)
                        known[k] = v
                        nwait += 1
                    ins = o.fn(E)
                    if o.sig is not None:
                        if o.cc:
                            ins.then_inc(sems[o.sig[0]])
                        else:
                            ins.then_inc(sems[o.sig[0]], 16 if o.dma else 1)
                if engname == final_wait_engine:
                    last = {}
                    for o in dmas:
                        k, v = o.sig
                        last[k] = max(last.get(k, 0), v)
                    for k, v in last.items():
                        if known.get(k, 0) < v:
                            E.wait_ge(sems[k], v)
                self.stats["waits_" + engname] = nwait

            block.tensor(lambda E: run("pe", E))
            block.scalar(lambda E: run("act", E))
            block.vector(lambda E: run("dve", E))
            block.gpsimd(lambda E: run("pool", E))
            block.sync(lambda E: run("sp", E))


def _consts(S):
    cb = np.zeros((128, NCB), np.float32)
    cb[:, CB_IDENT:CB_IDENT + 128] = np.eye(128)
    j = np.arange(128)[:, None]
    s = np.arange(128)[None, :]
    cb[:, CB_NEGTRI:CB_NEGTRI + 128] = np.where(j >= s, -1.0, 0.0)
    cb[:, CB_NEGONES:CB_NEGONES + 128] = -1.0
    cb[:, CB_ONES:CB_ONES + 128] = 1.0
    k = np.arange(128)[:, None]
    qrel = np.arange(896)[None, :] - 384
    in_diag = (qrel >= 0) & (qrel < 128)
    okA = np.where(qrel < 0, False, np.where(in_diag, (k // 64) <= (qrel // 64), True))
    okB = k < qrel
    okC = k <= qrel
    cb[:, CB_MASKA:CB_MASKA + 896] = np.where(okA, 0.0, NEG)
    cb[:, CB_MASKB:CB_MASKB + 896] = np.where(okB, 0.0, NEG)
    cb[:, CB_MASKC:CB_MASKC + 896] = np.where(okC, 0.0, NEG)
    cb = cb.astype(ml_dtypes.bfloat16)
    cf = np.zeros((128, NCF), np.float32)
    cf[:, CF_MEAN:CF_MEAN + 128] = 1.0 / 128.0
    cf[:, CF_ONES:CF_ONES + 128] = 1.0
    for h in range(4):
        cf[h, CF_SEL + h * 128:CF_SEL + (h + 1) * 128] = 1.0
        cf[h, CF_IDENT4 + h] = 1.0
    pos = np.arange(S, dtype=np.float32)
    inv_freq = (np.float32(ROPE_THETA) ** (-np.arange(0, 16, 2, dtype=np.float32) / np.float32(16))).astype(np.float32)
    ang = (pos[:, None] * inv_freq[None, :]).astype(np.float32)
    cos, sin = np.cos(ang).astype(np.float32), np.sin(ang).astype(np.float32)
    rope = np.zeros((2, 128, S), np.float32)
    rope[0] = 1.0
    for p in range(128):
        d = p % 64
        if d < 16:
            rope[0, p] = cos[:, d % 8]
            rope[1, p] = sin[:, d % 8]
    return cb, cf, rope


ARENA = KC * 4096


def alloc_bufs(nc, es, S):
    from types import SimpleNamespace
    NTB, NTT = S // 512, S // 128

    def sb(name, shape, dt):
        return es.enter_context(nc.sbuf_tensor("sb_" + name, shape, dt))
    B = SimpleNamespace()
    B.arena = sb("arena", [128, max(ARENA, KC * S)], BF16)
    B.hT = B.arena[:, 0:KC * S].rearrange("p (k s) -> p k s", k=KC)
    B.cb = sb("cb", [128, NCB], BF16)
    B.cf = sb("cf", [128, NCF], F32)
    B.stage = sb("stage", [128, 2, 1024], F32)
    B.Wb = sb("Wb", [128, KC, 512], BF16)
    B.Wrot = sb("Wrot", [128, KC, 256], BF16)
    B.Wffb = sb("Wffb", [128, KC, 4], BF16)
    B.wff_f = sb("wff_f", [128, KC, 4], F32)
    B.QT = sb("QT", [128, S], BF16)
    B.KTz = [sb(f"KT{i}z", [128, S], BF16) for i in range(2)]
    B.GT = sb("GT", [128, S], BF16)
    B.VA = sb("VA", [128, NTT, 128], BF16)
    B.VB = sb("VB", [128, NTT, 128], BF16)
    B.ropeb = [sb(f"ropeb{i}", [128, 2, 512], F32) for i in range(2)]
    B.gpre = sb("gpre", [128, KC], F32)
    B.rstd = sb("rstd", [128, NTT], F32)
    B.ssq = sb("ssq", [128, NTT], F32)
    B.small = sb("small", [128, 16], F32)
    B.lamt = sb("lamt", [128, 256], F32)
    B.lamp = sb("lamp", [128, 128], F32)
    B.subg = sb("subg", [128, 1], F32)
    B.fbt = sb("fbt", [4, 2], F32)
    B.cT = sb("cT", [4, S], F32)
    B.negck = sb("negck", [128, NTT, 4], F32)
    B.e_sb = [sb(f"e{i}", [128, 512], F32) for i in range(2)]
    B.sp2 = [sb(f"sp2_{i}", [128, 1024], BF16) for i in range(2)]
    B.sp_sb = [B.sp2[i][:, 0:512] for i in range(2)]
    B.pT2 = [sb(f"pT2_{i}", [128, 1024], BF16) for i in range(2)]
    B.pT_sb = [B.pT2[i // 2][:, (i % 2) * 512:(i % 2 + 1) * 512] for i in range(4)]
    B.ss2 = [sb(f"ss2_{i}", [128, 1024], F32) for i in range(2)]
    B.ss_sb = [B.ss2[i // 2][:, (i % 2) * 512:(i % 2 + 1) * 512] for i in range(4)]
    B.e2 = [B.e_sb[i][:].bitcast(BF16) for i in range(2)]
    B.xh = [B.e_sb[i][:].bitcast(BF16) for i in range(2)]
    B.sqj = B.ss_sb[0].bitcast(BF16)
    B.cqb_sb = [sb(f"cqb{i}", [128, 512], F32) for i in range(2)]
    B.tA = sb("tA", [128, 512], F32)
    B.tB = sb("tB", [128, 512], F32)
    B.tC = sb("tC", [128, 512], F32)
    B.fxe, B.fxs, B.ones4 = B.tA[0:4, :], B.tB[0:4, :], B.tC[0:4, :]
    B.drun = [sb(f"drun{i}", [128, 512], F32) for i in range(2)]
    B.y_sb = [sb(f"y{i}", [128, 512], BF16) for i in range(2)]
    B.st = sb("st", [128, 8], F32)
    B.ps2 = [es.enter_context(nc.psum_tensor(f"ps2_{i}", [128, 1024], F32)) for i in range(4)]
    B.ps = [B.ps2[i // 2][:, (i % 2) * 512:(i % 2 + 1) * 512] for i in range(8)]
    B.pst = [B.ps[6 + i].bitcast(BF16) for i in range(2)]
    B.ctr = {"rope": 0, "O": 0, "y": 0, "pT": 0, "S": 0, "e": 0, "cqb": 0, "ss": 0, "R": 0}
    return B


def emit_consts(P, B, cb_d, cf_d):
    P.dma("sp", lambda E: E.dma_start(out=B.cb[:], in_=cb_d), writes=["cb"])
    P.dma("sp", lambda E: E.dma_start(out=B.cf[:], in_=cf_d), writes=["cf"])
    P.pool(lambda E: E.memset(B.KTz[0][64:128, :], 0.0), writes=["KTzero"])
    P.pool(lambda E: E.memset(B.KTz[1][0:64, :], 0.0), writes=["KTzero"])


def build_attn(S, lambda_init, units=("A", "A", "B", "B", "C", "C")):
    nc = bass.Bass("TRN2", target_bir_lowering=False)
    x_d = nc.dram_tensor("x", [S, D_MODEL], F32, kind="ExternalInput").ap()
    prm = dict(
        wu=nc.dram_tensor("wu", [6, D_MODEL, 512], F32, kind="ExternalInput").ap(),
        wff=nc.dram_tensor("wff", [128, KC, 4], F32, kind="ExternalInput").ap(),
        gpre=nc.dram_tensor("gpre", [128, KC], F32, kind="ExternalInput").ap(),
        fb=nc.dram_tensor("fb", [4, 1], F32, kind="ExternalInput").ap(),
        lam=nc.dram_tensor("lam", [128, 256], F32, kind="ExternalInput").ap(),
        subg=nc.dram_tensor("subg", [128, 1], F32, kind="ExternalInput").ap())
    cb_d = nc.dram_tensor("cb", [128, NCB], BF16, kind="ExternalInput").ap()
    cf_d = nc.dram_tensor("cf", [128, NCF], F32, kind="ExternalInput").ap()
    rope_d = nc.dram_tensor("rope", [2, 128, S], F32, kind="ExternalInput").ap()
    yT_d = nc.dram_tensor("yT", [768, S], BF16, kind="ExternalOutput").ap()
    xv = x_d.rearrange("(t p) d -> t p d", p=128)
    with contextlib.ExitStack() as es:
        B = alloc_bufs(nc, es, S)
        P = Prog()
        emit_consts(P, B, cb_d, cf_d)
        emit_attn(P, B, S, lambda_init, units, lambda tt: (xv[tt], []), prm, rope_d,
                  lambda u, qb: (yT_d[u * 128:(u + 1) * 128, qb * 512:(qb + 1) * 512], []), lambda u: None)
        P.emit(nc)
        nc._prog_stats = P.stats
    return nc


def emit_attn(P, B, S, lambda_init, units, xsrc, prm, rope_d, ystore, unit_done, after_last_proj=None):
    NTB, NTT = S // 512, S // 128
    hT, cb, cf, stage, Wb, Wrot, Wffb, wff_f = B.hT, B.cb, B.cf, B.stage, B.Wb, B.Wrot, B.Wffb, B.wff_f
    QT, KTz, GT, VA, VB, ropeb, gpre, rstd, ssq, sqj, xh = B.QT, B.KTz, B.GT, B.VA, B.VB, B.ropeb, B.gpre, B.rstd, B.ssq, B.sqj, B.xh
    small, lamt, lamp, subg, fbt, cT, fxe, fxs, ones4, negck = B.small, B.lamt, B.lamp, B.subg, B.fbt, B.cT, B.fxe, B.fxs, B.ones4, B.negck
    e_sb, sp_sb, pT_sb, ss_sb, cqb_sb, tA, tB, tC, drun, y_sb = B.e_sb, B.sp_sb, B.pT_sb, B.ss_sb, B.cqb_sb, B.tA, B.tB, B.tC, B.drun, B.y_sb
    ps, pst = B.ps, B.pst
    PS = [f"ps{i}" for i in range(8)]
    wu_d, wff_d, gpre_d, fb_d, lam_d, subg_d = prm["wu"], prm["wff"], prm["gpre"], prm["fb"], prm["lam"], prm["subg"]

    ident = cb[:, CB_IDENT:CB_IDENT + 128]
    negtri = cb[:, CB_NEGTRI:CB_NEGTRI + 128]
    negones = cb[:, CB_NEGONES:CB_NEGONES + 128]
    maskoff = {"A": CB_MASKA, "B": CB_MASKB, "C": CB_MASKC}

    P.dma("sp", lambda E: E.dma_start(out=gpre[:], in_=gpre_d), writes=["gpre"])
    P.dma("sp", lambda E: E.dma_start(out=lamt[:], in_=lam_d), writes=["lamt"])
    P.dma("sp", lambda E: E.dma_start(out=subg[:], in_=subg_d), writes=["subg"])
    P.dma("sp", lambda E: E.dma_start(out=fbt[:, 0:1], in_=fb_d), writes=["fbt"])
    P.dma("sp", lambda E: E.dma_start(out=wff_f[:], in_=wff_d), writes=["wff_f"])
    P.dve(lambda E: E.tensor_tensor(out=lamp[:, 0:64], in0=lamt[:, 0:64], in1=lamt[:, 64:128], op=ALU.mult),
          reads=["lamt"], writes=["lamp"])
    P.dve(lambda E: E.tensor_tensor(out=lamp[:, 64:128], in0=lamt[:, 128:192], in1=lamt[:, 192:256], op=ALU.mult),
          reads=["lamt", "lamp"], writes=["lamp"])
    P.dve(lambda E: E.reduce_sum(out=small[:, 0:2], in_=lamp[:].rearrange("p (a b) -> p a b", a=2),
                                 axis=mybir.AxisListType.X), reads=["lamp"], writes=["small"])
    P.act(lambda E: E.activation(out=small[:, 2:4], in_=small[:, 0:2], func=AF.Exp), reads=["small"], writes=["small"])
    P.dve(lambda E: E.tensor_tensor(out=small[:, 4:5], in0=small[:, 3:4], in1=small[:, 2:3], op=ALU.subtract),
          reads=["small"], writes=["small"])
    P.dve(lambda E: E.tensor_scalar(out=small[:, 4:5], in0=small[:, 4:5], scalar1=-float(lambda_init), scalar2=None,
                                    op0=ALU.add), reads=["small"], writes=["small"])
    P.dve(lambda E: E.tensor_scalar(out=small[:, 5:6], in0=subg[:, 0:1], scalar1=float(1.0 - lambda_init), scalar2=None,
                                    op0=ALU.mult), reads=["small", "subg"], writes=["small"])
    neglam = small[:, 4:5]
    sgl = small[:, 5:6]
    P.dve(lambda E: E.tensor_scalar(out=fbt[:, 1:2], in0=fbt[:, 0:1], scalar1=-1.0, scalar2=None, op0=ALU.mult),
          reads=["fbt"], writes=["fbt"])
    P.dve(lambda E: E.tensor_tensor(out=Wffb[:], in0=wff_f[:], in1=gpre[:].unsqueeze(2).broadcast_to([128, KC, 4]),
                                    op=ALU.mult), reads=["wff_f", "gpre"], writes=["Wffb"])

    wv = wu_d.rearrange("u (kc p) c -> u p kc c", p=128)
    ctr = B.ctr
    _first = [u for u, t in enumerate(units) if t is not None][0]
    def load_weights(u, utype):
        for qtr in range(4):
            b = qtr % 2
            P.dma("sp", lambda E, qtr=qtr, b=b: E.dma_start(
                out=stage[:, b, :].rearrange("p (k c) -> p k c", k=2), in_=wv[u][:, qtr * 2:(qtr + 1) * 2, :]),
                writes=[f"stage{b}"])
            for k2 in range(2):
                kc = qtr * 2 + k2
                P.pool(lambda E, b=b, k2=k2, kc=kc: E.tensor_scalar(
                    out=Wb[:, kc, 0:128], in0=stage[:, b, k2 * 512:k2 * 512 + 128], scalar1=gpre[:, kc:kc + 1],
                    scalar2=0.125, op0=ALU.mult, op1=ALU.mult), reads=[f"stage{b}", "gpre"], writes=["Wb"])
                P.pool(lambda E, b=b, k2=k2, kc=kc: E.tensor_scalar(
                    out=Wb[:, kc, 128:512], in0=stage[:, b, k2 * 512 + 128:(k2 + 1) * 512],
                    scalar1=gpre[:, kc:kc + 1], scalar2=1.0, op0=ALU.mult, op1=ALU.mult), reads=[f"stage{b}", "gpre"],
                    writes=["Wb"])
        if utype == "A":
            P.pool(lambda E: E.memset(Wrot[:], 0.0), reads=[], writes=["Wrot"])
            for g in range(2):
                for m in range(2):
                    src0 = g * 128 + m * 64
                    dst0 = g * 128 + m * 64
                    P.pool(lambda E, s=src0, d=dst0: E.tensor_scalar(out=Wrot[:, :, d:d + 8], in0=Wb[:, :, s + 8:s + 16],
                                                                     scalar1=-1.0, scalar2=1.0, op0=ALU.mult,
                                                                     op1=ALU.mult),
                           reads=["Wb", "Wrot"], writes=["Wrot"])
                    P.pool(lambda E, s=src0, d=dst0: E.tensor_copy(out=Wrot[:, :, d + 8:d + 16], in_=Wb[:, :, s:s + 8]),
                           reads=["Wb", "Wrot"], writes=["Wrot"])

    load_weights(_first, units[_first])

    P.pool(lambda E: E.memset(small[:, 8:9], -0.5), writes=["small8"])

    def p1_load(tt):
        b = tt % 2
        xap, xkeys = xsrc(tt)
        P.dma("sp", lambda E: E.dma_start(out=stage[:, b, :], in_=xap), reads=xkeys, writes=[f"stage{b}"])
        P.act(lambda E: E.activation(out=sqj, in_=stage[:, b, :], func=AF.Square, accum_out=ssq[:, tt:tt + 1]),
              reads=[f"stage{b}"], writes=["ss0", f"ssq{b}"])
        P.pool(lambda E: E.tensor_scalar(out=rstd[:, tt:tt + 1], in0=ssq[:, tt:tt + 1], scalar1=1.0 / D_MODEL,
                                         scalar2=EPS, op0=ALU.mult, op1=ALU.add), reads=[f"ssq{b}"],
               writes=[f"rstd{b}"])
        P.pool(lambda E: E.tensor_tensor(out=rstd[:, tt:tt + 1], in0=rstd[:, tt:tt + 1], in1=small[:, 8:9],
                                         op=ALU.pow), reads=[f"rstd{b}", "small8"], writes=[f"rstd{b}"])

    def p1_scale(tt):
        b = tt % 2
        P.dve(lambda E: E.tensor_scalar(out=xh[b], in0=stage[:, b, :], scalar1=rstd[:, tt:tt + 1], scalar2=None,
                                        op0=ALU.mult), reads=[f"stage{b}", f"rstd{b}"], writes=[f"e{b}"])
        for kc in range(KC):
            P.pe(lambda E, kc=kc: E.transpose(out=pst[b][:, kc * 128:(kc + 1) * 128],
                                              in_=xh[b][:, kc * 128:(kc + 1) * 128], identity=ident),
                 reads=[f"e{b}", "cb"], writes=[f"ps{6 + b}"])

    def p1_evac(tt):
        b = tt % 2
        P.dve(lambda E: E.tensor_copy(out=hT[:, :, tt * 128:(tt + 1) * 128],
                                      in_=pst[b].rearrange("p (k t) -> p k t", k=KC)),
              reads=[f"ps{6 + b}"], writes=[f"hT{tt // 4}", "arena"])

    for step in range(NTT + 2):
        if step < NTT:
            p1_load(step)
        if 1 <= step <= NTT:
            p1_scale(step - 1)
        if step >= 2:
            p1_evac(step - 2)
    HT = [f"hT{i}" for i in range(NTB)]

    if "C" in units:
        P.pool(lambda E: E.memset(ones4, 1.0), writes=["tC"])
        for tb in range(NTB):
            pb = 4 + tb % 2
            for kc in range(KC):
                P.pe(lambda E, tb=tb, kc=kc, pb=pb: E.matmul(ps[pb][0:4, :], lhsT=Wffb[:, kc, :],
                                                           rhs=hT[:, kc, tb * 512:(tb + 1) * 512],
                                                           start=(kc == 0), stop=(kc == KC - 1)),
                     reads=["Wffb", HT[tb]], writes=[PS[pb]])
            P.act(lambda E, pb=pb: E.activation(out=fxe, in_=ps[pb][0:4, :], func=AF.Exp, bias=fbt[:, 1:2],
                                                scale=-1.0), reads=[PS[pb], "fbt"], writes=["tA"])
            P.act(lambda E: E.activation(out=fxs, in_=fxe, func=AF.Ln, bias=1.0, scale=1.0),
                  reads=["tA"], writes=["tB"])
            init = 0.0 if tb == 0 else cT[:, tb * 512 - 1:tb * 512]
            P.dve(lambda E, tb=tb, init=init: E.tensor_tensor_scan(out=cT[:, tb * 512:(tb + 1) * 512], data0=ones4,
                                                                   data1=fxs, initial=init, op0=ALU.mult,
                                                                   op1=ALU.subtract),
                  reads=["tB", "tC", "cT"], writes=["cT"])
        for kb in range(NTT):
            P.pe(lambda E, kb=kb: E.transpose(out=ps[5][:, kb * 4:(kb + 1) * 4], in_=cT[:, kb * 128:(kb + 1) * 128],
                                              identity=cf[0:4, CF_IDENT4:CF_IDENT4 + 4]),
                 reads=["cT", "cf"], writes=[PS[5]])
        P.dve(lambda E: E.tensor_scalar(out=negck[:].rearrange("p a b -> p (a b)"), in0=ps[5][:, 0:NTT * 4],
                                        scalar1=-1.0, scalar2=None, op0=ALU.mult), reads=[PS[5]], writes=["negck"])

    ones_done = [False]
    zeros_done = [False]

    def proj_group(wt, c0, tb, pb, wkey):
        for kc in range(KC):
            P.pe(lambda E, kc=kc: E.matmul(ps[pb][:], lhsT=wt[:, kc, c0:c0 + 128], rhs=hT[:, kc, tb * 512:(tb + 1) * 512],
                                           start=(kc == 0), stop=(kc == KC - 1)),
                 reads=[wkey, HT[tb]], writes=[PS[pb]])

    def projections(utype):
        for tb in range(NTB):
            sl = slice(tb * 512, (tb + 1) * 512)
            if utype == "A":
                rb = ctr["rope"] % 2
                ctr["rope"] += 1
                P.dma("sp", lambda E, rb=rb, sl=sl: E.dma_start(out=ropeb[rb][:], in_=rope_d[:, :, sl].rearrange("a p s -> p a s")),
                      writes=[f"ropeb{rb}"])
                for g, dkey in enumerate(("QT", "KT")):
                    proj_group(Wb, g * 128, tb, 0, "Wb")
                    proj_group(Wrot, g * 128, tb, 1, "Wrot")
                    P.dve(lambda E, rb=rb: E.tensor_tensor(out=tA[:], in0=ps[0][:], in1=ropeb[rb][:, 0, :], op=ALU.mult),
                          reads=[PS[0], f"ropeb{rb}"], writes=["tA"])
                    P.dve(lambda E, rb=rb: E.tensor_tensor(out=tB[:], in0=ps[1][:], in1=ropeb[rb][:, 1, :], op=ALU.mult),
                          reads=[PS[1], f"ropeb{rb}"], writes=["tB"])
                    if g == 0:
                        P.pool(lambda E, sl=sl: E.tensor_tensor(out=QT[:, sl], in0=tA[:], in1=tB[:], op=ALU.add),
                               reads=["tA", "tB"], writes=[f"QT{tb}"])
                    else:
                        for mm in range(2):
                            P.pool(lambda E, sl=sl, mm=mm: E.tensor_tensor(
                                out=KTz[mm][mm * 64:(mm + 1) * 64, sl], in0=tA[mm * 64:(mm + 1) * 64, :],
                                in1=tB[mm * 64:(mm + 1) * 64, :], op=ALU.add),
                                reads=["tA", "tB", "KTzero"], writes=[f"KT{tb}"])
            else:
                proj_group(Wb, 0, tb, 0, "Wb")
                P.act(lambda E, sl=sl: E.activation(out=QT[:, sl], in_=ps[0][:], func=AF.Copy), reads=[PS[0]],
                      writes=[f"QT{tb}"])
                proj_group(Wb, 128, tb, 1, "Wb")
                for mm in range(2):
                    P.dve(lambda E, sl=sl, mm=mm: E.tensor_copy(out=KTz[mm][mm * 64:(mm + 1) * 64, sl],
                                                               in_=ps[1][mm * 64:(mm + 1) * 64, :]),
                          reads=[PS[1], "KTzero"], writes=[f"KT{tb}"])
            proj_group(Wb, 384, tb, 2, "Wb")
            P.act(lambda E, sl=sl: E.activation(out=GT[:, sl], in_=ps[2][:], func=AF.Silu), reads=[PS[2]],
                  writes=[f"GT{tb}"])
            for t4 in range(4):
                tt = tb * 4 + t4
                for kc in range(KC):
                    P.pe(lambda E, tt=tt, t4=t4, kc=kc: E.matmul(ps[3][:, t4 * 128:(t4 + 1) * 128],
                                                                lhsT=hT[:, kc, tt * 128:(tt + 1) * 128],
                                                                rhs=Wb[:, kc, 256:384], start=(kc == 0),
                                                                stop=(kc == KC - 1)),
                         reads=["Wb", HT[tb]], writes=[PS[3]])
            pv = ps[3][:].rearrange("p (t c) -> p t c", t=4)
            if utype in ("B", "C"):
                d1 = 0 if utype == "C" else 64
                P.dve(lambda E, tb=tb, pv=pv: E.tensor_copy(out=VA[:, tb * 4:(tb + 1) * 4, 0:64], in_=pv[:, :, 0:64]),
                      reads=[PS[3]], writes=[f"VA{tb}"])
                P.dve(lambda E, tb=tb, pv=pv, d1=d1: E.tensor_copy(out=VB[:, tb * 4:(tb + 1) * 4, d1:d1 + 64],
                                                                 in_=pv[:, :, 64:128]),
                      reads=[PS[3]], writes=[f"VB{tb}"])
            else:
                P.dve(lambda E, tb=tb, pv=pv: E.tensor_copy(out=VA[:, tb * 4:(tb + 1) * 4, :], in_=pv),
                      reads=[PS[3]], writes=[f"VA{tb}"])

    def qk(utype, m, kb, qb, sbank):
        j = kb - 4 * qb
        diag = j >= 0
        c0 = 128 * j if j > 0 else 0
        P.pe(lambda E: E.matmul(ps[sbank][:, c0:512], lhsT=KTz[m][:, kb * 128:(kb + 1) * 128],
                                rhs=QT[:, qb * 512 + c0:(qb + 1) * 512], start=True, stop=True),
             reads=[f"KT{kb // 4}", f"QT{qb}"], writes=[PS[sbank]])
        if diag:
            mo = maskoff[utype] + 384
            P.pe(lambda E: E.matmul(ps[sbank][:, c0:c0 + 128], lhsT=ident, rhs=cb[:, mo:mo + 128], start=False, stop=True,
                                    skip_group_check=True), reads=["cb", PS[sbank]], writes=[PS[sbank]])
        return c0

    def out_dma(u, qb, yb):
        yap, ykeys = ystore(u, qb)
        P.dma("sp", lambda E: E.dma_start(out=yap, in_=y_sb[yb][:]), reads=[f"y{yb}"], writes=ykeys)

    pending = []
    cur_step = [0, 0]

    def defer(k, fn):
        cur_step[1] += 1
        pending.append((cur_step[0] + k, cur_step[1], fn))

    def run_pipeline(items, stages, lags):
        n, L = len(items), max(lags)
        del pending[:]
        step = 0
        while step < n + L or pending:
            cur_step[0] = step
            for f, lag in zip(stages, lags):
                i = step - lag
                if 0 <= i < n:
                    f(items[i])
            due = sorted([p for p in pending if p[0] <= step])
            for p in due:
                pending.remove(p)
            for p in due:
                p[2]()
            step += 1

    def make_items(order_desc=False):
        items = []
        for qb in range(NTB):
            nk = 4 * qb + 4
            for m in range(2):
                for i in range(nk):
                    items.append(dict(qb=qb, m=m, i=i, nk=nk, kb=(nk - 1 - i) if order_desc else i))
        return items

    ones_bf = cb[:, CB_ONES:CB_ONES + 128]

    def attn_A(u):
        items = make_items()
        grp = {}

        def st1(it):
            qb, m, i = it["qb"], it["m"], it["i"]
            if i == 0:
                grp[(qb, m)] = 2 + ctr["O"] % 2
                ctr["O"] += 1
            it["ob"] = grp[(qb, m)]
            sbank = (0, 1, 6, 7)[ctr["S"] % 4]
            ctr["S"] += 1
            it["sbank"] = sbank
            it["c0"] = qk("A", m, it["kb"], qb, sbank)

        def st2(it):
            sbank, c0 = it["sbank"], it["c0"]
            pb = ctr["pT"] % 4
            ctr["pT"] += 1
            it["pb"] = pb
            P.act(lambda E: E.activation(out=pT_sb[pb][:, c0:512], in_=ps[sbank][:, c0:512], func=AF.Exp),
                  reads=[PS[sbank]], writes=[f"pT{pb}"])

        def st3(it):
            qb, m, i, nk, kb, ob, pb = it["qb"], it["m"], it["i"], it["nk"], it["kb"], it["ob"], it["pb"]
            c0 = it["c0"]
            P.pe(lambda E: E.matmul(ps[ob][:, c0:512], lhsT=VA[:, kb, :], rhs=pT_sb[pb][:, c0:512], start=(i == 0),
                                    stop=(i == nk - 1)),
                 reads=[f"VA{kb // 4}", f"pT{pb}"], writes=[PS[ob]])
            g = it["g"]
            dset = (drun[0], drun[1]) if g % 2 == 0 else (cqb_sb[0], cqb_sb[1])
            dkeys = ("drun0", "drun1") if g % 2 == 0 else ("cqb0", "cqb1")
            if i % 4 == 3:
                P.pe(lambda E: E.matmul(ps[4][:, c0:512], lhsT=ones_bf, rhs=pT_sb[pb][:, c0:512], start=(i == 3),
                                        stop=False, skip_group_check=True),
                     reads=["cb", f"pT{pb}"], writes=[PS[4]])
                if i == nk - 1:
                    group_end_A(u, qb, m, ob, g, dset, dkeys)
                return
            par = (i - i // 4) % 2
            eng = P.dve
            dt_, dk_ = dset[par], dkeys[par]
            if i < 2:
                if c0 > 0:
                    eng(lambda E: E.memset(dt_[:, 0:c0], 0.0), writes=[dk_])
                eng(lambda E: E.tensor_copy(out=dt_[:, c0:512], in_=pT_sb[pb][:, c0:512]), reads=[f"pT{pb}"],
                    writes=[dk_])
            else:
                eng(lambda E: E.tensor_tensor(out=dt_[:, c0:512], in0=dt_[:, c0:512], in1=pT_sb[pb][:, c0:512],
                                              op=ALU.add),
                    reads=[f"pT{pb}", dk_], writes=[dk_])
            if i == nk - 1:
                group_end_A(u, qb, m, ob, g, dset, dkeys)

        def group_end_A(u, qb, m, ob, g, dset, dkeys):
            tD, tDk = e_sb[0], "e0"
            qsl = slice(qb * 512, (qb + 1) * 512)
            dsum, dsk = sp_sb[g % 2], f"sp{g % 2}"
            defer(0, lambda: P.dve(lambda E: E.tensor_tensor(out=dsum, in0=dset[0][:], in1=dset[1][:], op=ALU.add),
                                   reads=[dkeys[0], dkeys[1]], writes=[dsk]))
            defer(1, lambda: P.pe(lambda E: E.matmul(ps[4][:], lhsT=ones_bf, rhs=dsum, start=False, stop=True,
                                                     skip_group_check=True),
                                  reads=["cb", dsk, PS[4]], writes=[PS[4]]))
            defer(2, lambda: P.act(lambda E: E.activation(out=tA[:], in_=ps[4][:], func=AF.Ln), reads=[PS[4]],
                                   writes=["tA"]))
            defer(3, lambda: P.act(lambda E: E.activation(out=tA[:], in_=tA[:], func=AF.Exp, scale=-1.0),
                                   reads=["tA"], writes=["tA"]))
            if m == 0:
                defer(4, lambda: P.dve(lambda E: E.tensor_tensor(out=tB[:], in0=ps[ob][:], in1=tA[:], op=ALU.mult),
                                       reads=[PS[ob], "tA"], writes=["tB"]))
                return
            yb = ctr["y"] % 2
            ctr["y"] += 1
            defer(4, lambda: P.dve(lambda E: E.tensor_tensor(out=tC[:], in0=ps[ob][:], in1=tA[:], op=ALU.mult),
                                   reads=[PS[ob], "tA"], writes=["tC"]))
            defer(5, lambda: P.dve(lambda E: E.scalar_tensor_tensor(out=tB[:], in0=tC[:], scalar=neglam, in1=tB[:],
                                                                    op0=ALU.mult, op1=ALU.add),
                                   reads=["tB", "tC", "small"], writes=["tB"]))
            defer(6, lambda: P.pool(lambda E: E.tensor_tensor(out=tC[:], in0=tB[:], in1=tB[:], op=ALU.mult),
                                    reads=["tB"], writes=["tC"]))
            defer(7, lambda: P.pe(lambda E: E.matmul(ps[5][:], lhsT=cf[:, CF_MEAN:CF_MEAN + 128], rhs=tC[:], start=True,
                                                     stop=True), reads=["cf", "tC"], writes=[PS[5]]))
            defer(8, lambda: P.dve(lambda E: E.tensor_scalar(out=tD[:], in0=ps[5][:], scalar1=EPS, scalar2=None,
                                                             op0=ALU.add), reads=[PS[5]], writes=[tDk]))
            defer(9, lambda: P.act(lambda E: E.activation(out=tD[:], in_=tD[:], func=AF.Ln), reads=[tDk], writes=[tDk]))
            defer(10, lambda: P.act(lambda E: E.activation(out=tD[:], in_=tD[:], func=AF.Exp, scale=-0.5), reads=[tDk],
                                    writes=[tDk]))
            defer(11, lambda: P.dve(lambda E: E.scalar_tensor_tensor(out=tC[:], in0=tB[:], scalar=sgl, in1=tD[:],
                                                                     op0=ALU.mult, op1=ALU.mult),
                                    reads=["tB", tDk, "small"], writes=["tC"]))
            defer(12, lambda: P.dve(lambda E: E.tensor_tensor(out=y_sb[yb][:], in0=tC[:], in1=GT[:, qsl], op=ALU.mult),
                                    reads=["tC", f"GT{qb}"], writes=[f"y{yb}"]))
            defer(13, lambda: out_dma(u, qb, yb))

        for gi, it in enumerate(items):
            pass
        gidx = {}
        for it in items:
            key = (it["qb"], it["m"])
            if key not in gidx:
                gidx[key] = len(gidx)
            it["g"] = gidx[key]
        run_pipeline(items, [st1, st2, st3], [0, 2, 3])

    def attn_C(u, jpair):
        items = make_items()
        grp = {}
        SB3 = (0, 1, 5, 6)
        glist = []
        for it in items:
            key = (it["qb"], it["m"])
            if key not in glist:
                glist.append(key)

        def prep_group(gi):
            if gi >= len(glist) or glist[gi] in grp:
                return
            qb, m = glist[gi]
            h = jpair * 2 + m
            qsl = slice(qb * 512, (qb + 1) * 512)
            if m == 0:
                grp[("y", qb)] = ctr["y"] % 2
                ctr["y"] += 1
            ob = 2 + ctr["O"] % 2
            ctr["O"] += 1
            cq = ctr["cqb"] % 2
            ctr["cqb"] += 1
            grp[(qb, m)] = (ob, cq)
            P.pe(lambda E: E.matmul(ps[4][:], lhsT=cf[0:4, CF_SEL + h * 128:CF_SEL + (h + 1) * 128],
                                    rhs=cT[:, qsl], start=True, stop=True),
                 reads=["cf", "cT"], writes=[PS[4]])
            P.act(lambda E: E.activation(out=cqb_sb[cq][:], in_=ps[4][:], func=AF.Copy), reads=[PS[4]],
                  writes=[f"cqb{cq}"])

        def st1(it):
            qb, m, i = it["qb"], it["m"], it["i"]
            h = jpair * 2 + m
            gi = glist.index((qb, m))
            if i == 0:
                prep_group(gi)
            if i == 1 or (i == 0 and gi == 0):
                prep_group(gi + 1)
            ob, cq = grp[(qb, m)]
            it["ob"], it["h"], it["yb"] = ob, h, grp[("y", qb)]
            sbank = SB3[ctr["S"] % 4]
            ctr["S"] += 1
            c0 = qk("C", m, it["kb"], qb, sbank)
            it["c0"] = c0
            sidx = ctr["ss"] % 4
            ctr["ss"] += 1
            it["sidx"] = sidx
            kb = it["kb"]
            P.dve(lambda E: E.scalar_tensor_tensor(out=ss_sb[sidx][:, c0:512], in0=ps[sbank][:, c0:512],
                                                   scalar=negck[:, kb, h:h + 1], in1=cqb_sb[cq][:, c0:512],
                                                   op0=ALU.add, op1=ALU.add),
                  reads=[PS[sbank], f"cqb{cq}", "negck"], writes=[f"ss{sidx}"])

        def st2(it):
            sidx, c0 = it["sidx"], it["c0"]
            pb = ctr["pT"] % 4
            ctr["pT"] += 1
            it["pb"] = pb
            P.act(lambda E: E.activation(out=pT_sb[pb][:, c0:512], in_=ss_sb[sidx][:, c0:512], func=AF.Exp),
                  reads=[f"ss{sidx}"], writes=[f"pT{pb}"])

        def st3(it):
            qb, m, i, nk, kb, ob, pb, yb = it["qb"], it["m"], it["i"], it["nk"], it["kb"], it["ob"], it["pb"], it["yb"]
            Vh = VA if m == 0 else VB
            vkey = "VA" if m == 0 else "VB"
            c0 = it["c0"]
            P.pe(lambda E: E.matmul(ps[ob][:, c0:512], lhsT=Vh[:, kb, :], rhs=pT_sb[pb][:, c0:512], start=(i == 0),
                                    stop=(i == nk - 1)),
                 reads=[f"{vkey}{kb // 4}", f"pT{pb}"], writes=[PS[ob]])
            if i == nk - 1:
                group_end_C(qb, m, ob, yb)

        def group_end_C(qb, m, ob, yb):
            r0 = m * 64
            qsl = slice(qb * 512, (qb + 1) * 512)
            defer(1, lambda: P.act(lambda E: E.activation(out=tA[64:128, :], in_=ps[ob][64:128, :], func=AF.Ln),
                                   reads=[PS[ob]], writes=["tA"]))
            defer(2, lambda: P.act(lambda E: E.activation(out=tA[64:128, :], in_=tA[64:128, :], func=AF.Exp, scale=-1.0),
                                   reads=["tA"], writes=["tA"]))
            defer(3, lambda: P.dve(lambda E: E.tensor_tensor(out=tB[r0:r0 + 64, :], in0=ps[ob][0:64, :],
                                                             in1=tA[64:128, :], op=ALU.mult),
                                   reads=[PS[ob], "tA"], writes=["tB"]))
            defer(4, lambda: P.dve(lambda E: E.tensor_tensor(out=y_sb[yb][r0:r0 + 64, :], in0=tB[r0:r0 + 64, :],
                                                             in1=GT[r0:r0 + 64, qsl], op=ALU.mult),
                                   reads=["tB", f"GT{qb}"], writes=[f"y{yb}"]))
            if m == 1:
                defer(5, lambda: out_dma(u, qb, yb))

        run_pipeline(items, [st1, st2, st3], [0, 1, 2])

    def attn_B(u):
        sp2, pT2, ss2, e2, ps2 = B.sp2, B.pT2, B.ss2, B.e2, B.ps2
        items = []
        for qb in range(NTB):
            nk = 4 * qb + 4
            for m in range(2):
                for p in range(nk // 2):
                    items.append(dict(qb=qb, m=m, p=p, np=nk // 2, kba=nk - 1 - 2 * p, kbb=nk - 2 - 2 * p))
        grp = {}
        Ra, Rb = cqb_sb[0], cqb_sb[1]

        def act_pair(func, out2, in2, c0a, c0b, reads, writes, **kw):
            if c0a == 0 and c0b == 0:
                P.act(lambda E: E.activation(out=out2[:, 0:1024], in_=in2[:, 0:1024], func=func, **kw), reads=reads,
                      writes=writes)
            else:
                P.act(lambda E: E.activation(out=out2[:, c0a:512], in_=in2[:, c0a:512], func=func, **kw), reads=reads,
                      writes=writes)
                P.act(lambda E: E.activation(out=out2[:, 512 + c0b:1024], in_=in2[:, 512 + c0b:1024], func=func, **kw),
                      reads=reads, writes=writes)

        def s1(it):
            qb, m, p = it["qb"], it["m"], it["p"]
            if p == 0 and m == 0:
                yb = ctr["y"] % 2
                ctr["y"] += 1
                grp[qb] = yb
            it["yb"] = grp[qb]
            x = ctr["S"] % 3
            ctr["S"] += 1
            it["x"] = x
            c0a = qk("B", m, it["kba"], qb, 2 * x)
            c0b = qk("B", m, it["kbb"], qb, 2 * x + 1)
            it["c0a"], it["c0b"] = c0a, c0b
            ei = ctr["e"] % 2
            ctr["e"] += 1
            it["ei"] = ei
            act_pair(AF.Exp, e2[ei], ps2[x], c0a, c0b, [PS[2 * x], PS[2 * x + 1]], [f"e{ei}"])

        def s2(it):
            ei = it["ei"]
            act_pair(AF.Ln, sp2[ei], e2[ei], it["c0a"], it["c0b"], [f"e{ei}"], [f"sp{ei}"], bias=1.0, scale=1.0)

        def s3(it):
            p, np_, x, ei, c0a, c0b = it["p"], it["np"], it["x"], it["ei"], it["c0a"], it["c0b"]
            last = (p == np_ - 1)
            ba, bb = 2 * x, 2 * x + 1
            spa, spb = sp2[ei][:, c0a:512], sp2[ei][:, 512 + c0b:1024]
            si = ctr["ss"] % 2
            ctr["ss"] += 1
            ssk = [f"ss{2 * si}", f"ss{2 * si + 1}"]
            P.pe(lambda E: E.matmul(ps[ba][:, c0a:512], lhsT=negtri, rhs=spa, start=False, stop=True,
                                    skip_group_check=True),
                 reads=["cb", f"sp{ei}", PS[ba]], writes=[PS[ba]])
            P.pe(lambda E: E.matmul(ps[bb][:, c0b:512], lhsT=negtri, rhs=spb, start=False, stop=True,
                                    skip_group_check=True),
                 reads=["cb", f"sp{ei}", PS[bb]], writes=[PS[bb]])
            if p == 0:
                P.pool(lambda E: E.memset(Ra[:], 0.0), writes=["cqb0"])
                P.pool(lambda E: E.memset(Rb[:], 0.0), writes=["cqb1"])
                P.dve(lambda E: E.tensor_copy(out=ss2[si][:, c0a:512], in_=ps[ba][:, c0a:512]), reads=[PS[ba]],
                      writes=[ssk[0]])
            else:
                P.dve(lambda E: E.tensor_tensor(out=ss2[si][:, c0a:512], in0=ps[ba][:, c0a:512], in1=Rb[:, c0a:512],
                                                op=ALU.add), reads=[PS[ba], "cqb1"], writes=[ssk[0]])
            P.pe(lambda E: E.matmul(ps[6][:, c0a:512], lhsT=negones, rhs=spa, start=(p == 0), stop=True,
                                    skip_group_check=True),
                 reads=["cb", f"sp{ei}"], writes=[PS[6]])
            P.dve(lambda E: E.tensor_copy(out=Ra[:, c0a:512], in_=ps[6][:, c0a:512]), reads=[PS[6]], writes=["cqb0"])
            P.dve(lambda E: E.tensor_tensor(out=ss2[si][:, 512 + c0b:1024], in0=ps[bb][:, c0b:512], in1=Ra[:, c0b:512],
                                            op=ALU.add), reads=[PS[bb], "cqb0"], writes=[ssk[1]])
            if not last:
                P.pe(lambda E: E.matmul(ps[6][:, c0b:512], lhsT=negones, rhs=spb, start=False, stop=True,
                                        skip_group_check=True),
                     reads=["cb", f"sp{ei}", PS[6]], writes=[PS[6]])
                P.dve(lambda E: E.tensor_copy(out=Rb[:, c0b:512], in_=ps[6][:, c0b:512]), reads=[PS[6]], writes=["cqb1"])
            pb = ctr["pT"] % 2
            ctr["pT"] += 1
            it["pb"] = pb
            act_pair(AF.Exp, pT2[pb], ss2[si], c0a, c0b, ssk, [f"pT{2 * pb}", f"pT{2 * pb + 1}"])

        def s4(it):
            qb, m, p, np_, pb, yb = it["qb"], it["m"], it["p"], it["np"], it["pb"], it["yb"]
            kba, kbb, c0a, c0b = it["kba"], it["kbb"], it["c0a"], it["c0b"]
            Vm = VA if m == 0 else VB
            vkey = "VA" if m == 0 else "VB"
            pk = [f"pT{2 * pb}", f"pT{2 * pb + 1}"]
            P.pe(lambda E: E.matmul(ps[7][:, c0a:512], lhsT=Vm[:, kba, :], rhs=pT2[pb][:, c0a:512],
                                    start=(m == 0 and p == 0), stop=False, skip_group_check=True),
                 reads=[f"{vkey}{kba // 4}"] + pk, writes=[PS[7]])
            P.pe(lambda E: E.matmul(ps[7][:, c0b:512], lhsT=Vm[:, kbb, :], rhs=pT2[pb][:, 512 + c0b:1024],
                                    start=False, stop=(m == 1 and p == np_ - 1), skip_group_check=True),
                 reads=[f"{vkey}{kbb // 4}"] + pk, writes=[PS[7]])
            if m == 1 and p == np_ - 1:
                qsl = slice(qb * 512, (qb + 1) * 512)
                defer(0, lambda: P.dve(lambda E: E.tensor_tensor(out=y_sb[yb][:], in0=ps[7][:], in1=GT[:, qsl],
                                                                 op=ALU.mult),
                                       reads=[PS[7], f"GT{qb}"], writes=[f"y{yb}"]))
                defer(1, lambda: out_dma(u, qb, yb))

        run_pipeline(items, [s1, s2, s3, s4], [0, 1, 2, 3])

    jc = 0
    ulist = [u for u, t in enumerate(units) if t is not None]
    for ui, u in enumerate(ulist):
        utype = units[u]
        allk = [f"VA{i}" for i in range(NTB)]
        allkb = [f"VB{i}" for i in range(NTB)]
        if utype == "B" and not zeros_done[0]:
            zeros_done[0] = True
            P.pool(lambda E: E.memset(VA[:, :, 64:128], 0.0), writes=allk)
            P.pool(lambda E: E.memset(VB[:, :, 0:64], 0.0), writes=allkb)
        if utype == "C" and not ones_done[0]:
            ones_done[0] = True
            P.pool(lambda E: E.memset(VA[:, :, 64:128], 1.0), writes=allk)
            P.pool(lambda E: E.memset(VB[:, :, 64:128], 1.0), writes=allkb)
        projections(utype)
        if ui + 1 == len(ulist) and after_last_proj is not None:
            after_last_proj()
        if ui + 1 < len(ulist):
            load_weights(ulist[ui + 1], units[ulist[ui + 1]])
        if utype == "A":
            attn_A(u)
        elif utype == "B":
            attn_B(u)
        else:
            attn_C(u, jc)
            jc += 1
        unit_done(u)


KO = 12


def emit_outproj(P, B, S, ydst, xsrc, wo_d, gp_d, ostore, phase="all"):
    NTT = S // 128
    ar = B.arena
    wo_b = ar[:, 0:12288].rearrange("p (k d) -> p k d", k=KO)
    wo_f = ar[:, 12288:16384].bitcast(F32).rearrange("p (a d) -> p a d", a=2)
    gpost = ar[:, 16384:18432].bitcast(F32)
    yt = [ar[:, 18432 + i * 3072:18432 + (i + 1) * 3072].rearrange("p (k t) -> p k t", k=KO) for i in range(2)]
    xt = [ar[:, 24576 + i * 2048:24576 + (i + 1) * 2048].bitcast(F32) for i in range(2)]
    ot = [ar[:, 28672 + i * 2048:28672 + (i + 1) * 2048].bitcast(F32) for i in range(2)]
    st = B.st
    pso = B.ps2
    NTB = S // 512
    AR = ["arena"]
    if phase in ("all", "prefetch"):
        P.pool(lambda E: E.memset(st[:, 7:8], 0.0), writes=[f"hT{i}" for i in range(NTB)] + ["arena"])
        P.dma("pool", lambda E: E.dma_start(out=gpost, in_=gp_d), reads=AR, writes=["gpost"])
        for kc in range(KO):
            b = kc % 2
            P.dma("pool", lambda E, kc=kc, b=b: E.dma_start(out=wo_f[:, b, :], in_=wo_d[kc]), reads=AR,
                  writes=[f"wo_f{b}"])
            P.pool(lambda E, kc=kc, b=b: E.tensor_copy(out=wo_b[:, kc, :], in_=wo_f[:, b, :]), reads=[f"wo_f{b}"] + AR,
                   writes=["wo_b"])
        if phase == "prefetch":
            return
    for c in range(S // 256):
        yb = c % 2
        for u in range(6):
            P.dma("sp", lambda E, u=u, c=c, yb=yb: E.dma_start(
                out=yt[yb][:, 2 * u:2 * u + 2, :],
                in_=ydst[u].rearrange("(r p) t -> p r t", p=128)[:, :, c * 256:(c + 1) * 256]),
                reads=[f"ydst{u}"] + AR, writes=[f"yt{yb}"])
        for t2 in range(2):
            tt = c * 2 + t2
            b = tt % 2
            pk = [f"ps{2 * b}", f"ps{2 * b + 1}"]
            xap, xkeys = xsrc(tt)
            P.dma("sp", lambda E, xap=xap, b=b: E.dma_start(out=xt[b], in_=xap), reads=xkeys + AR, writes=[f"xt{b}"])
            for half in range(2):
                for kc in range(KO):
                    P.pe(lambda E, b=b, half=half, kc=kc, yb=yb, t2=t2: E.matmul(
                        pso[b][:, half * 512:(half + 1) * 512], lhsT=yt[yb][:, kc, t2 * 128:(t2 + 1) * 128],
                        rhs=wo_b[:, kc, half * 512:(half + 1) * 512], start=(kc == 0), stop=(kc == KO - 1)),
                        reads=[f"yt{yb}", "wo_b"] + AR, writes=[pk[half]])
            c0 = 3 * b
            sk = f"st{b}"
            P.act(lambda E, b=b, c0=c0: E.activation(out=ot[b], in_=pso[b][:], func=AF.Square,
                                                     accum_out=st[:, c0:c0 + 1]),
                  reads=pk + AR, writes=[f"ot{b}", sk])
            P.dve(lambda E, c0=c0: E.tensor_scalar(out=st[:, c0 + 1:c0 + 2], in0=st[:, c0:c0 + 1], scalar1=1.0 / D_MODEL,
                                                   scalar2=EPS, op0=ALU.mult, op1=ALU.add), reads=[sk], writes=[sk])
            P.act(lambda E, c0=c0: E.activation(out=st[:, c0 + 1:c0 + 2], in_=st[:, c0 + 1:c0 + 2], func=AF.Ln),
                  reads=[sk], writes=[sk])
            P.act(lambda E, c0=c0: E.activation(out=st[:, c0 + 2:c0 + 3], in_=st[:, c0 + 1:c0 + 2], func=AF.Exp,
                                                scale=-0.5), reads=[sk], writes=[sk])
            P.dve(lambda E, b=b, c0=c0: E.scalar_tensor_tensor(out=ot[b], in0=pso[b][:], scalar=st[:, c0 + 2:c0 + 3],
                                                               in1=gpost, op0=ALU.mult, op1=ALU.mult),
                  reads=pk + [sk, "gpost"] + AR, writes=[f"ot{b}"])
            P.pool(lambda E, b=b: E.tensor_tensor(out=ot[b], in0=ot[b], in1=xt[b], op=ALU.add),
                   reads=[f"ot{b}", f"xt{b}"] + AR, writes=[f"ot{b}"])
            oap, okeys = ostore(tt)
            P.dma("pool", lambda E, oap=oap, b=b: E.dma_start(out=oap, in_=ot[b]), reads=[f"ot{b}"] + AR, writes=okeys)


PAIRS = [[0, 1], [2, 3], [4, 5], [6, 7]]
UNITS = ("A", "A", "B", "B", "C", "C")


def build_fused(S, depth=DEPTH):
    nc = bass.Bass("TRN2", target_bir_lowering=False)
    x_d = nc.dram_tensor("x", [S, D_MODEL], F32, kind="ExternalInput").ap()
    wu_d = nc.dram_tensor("wu", [depth, 6, D_MODEL, 512], F32, kind="ExternalInput").ap()
    wff_d = nc.dram_tensor("wff", [depth, 128, KC, 4], F32, kind="ExternalInput").ap()
    gpre_d = nc.dram_tensor("gpre", [depth, 128, KC], F32, kind="ExternalInput").ap()
    fb_d = nc.dram_tensor("fb", [depth, 4, 1], F32, kind="ExternalInput").ap()
    lam_d = nc.dram_tensor("lam", [depth, 128, 256], F32, kind="ExternalInput").ap()
    subg_d = nc.dram_tensor("subg", [depth, 128, 1], F32, kind="ExternalInput").ap()
    wo_d = nc.dram_tensor("wo", [depth, KO, 128, D_MODEL], F32, kind="ExternalInput").ap()
    gp_d = nc.dram_tensor("gpost", [depth, 128, D_MODEL], F32, kind="ExternalInput").ap()
    cb_d = nc.dram_tensor("cb", [128, NCB], BF16, kind="ExternalInput").ap()
    cf_d = nc.dram_tensor("cf", [128, NCF], F32, kind="ExternalInput").ap()
    rope_d = nc.dram_tensor("rope", [2, 128, S], F32, kind="ExternalInput").ap()
    o_d = nc.dram_tensor("xo", [S, D_MODEL], F32, kind="ExternalOutput").ap()
    ysrc = [[nc.dram_tensor(f"ysrc{l}_{u}", [128, S], BF16) for u in range(6)] for l in range(depth)]
    ydst = [[nc.dram_tensor(f"ydst{l}_{u}", [256, S], BF16) for u in range(6)] for l in range(depth)]
    xmid = [nc.dram_tensor(f"xmid{l}", [S, D_MODEL], F32) for l in range(depth - 1)]
    NTB = S // 512
    with contextlib.ExitStack() as es:
        B = alloc_bufs(nc, es, S)
        P = Prog()
        emit_consts(P, B, cb_d, cf_d)
        for l in range(depth):
            lam_init = 0.8 - 0.6 * math.exp(-0.3 * l)
            if l == 0:
                xv = x_d.rearrange("(t p) d -> t p d", p=128)

                def xsrc(tt, xv=xv):
                    return xv[tt], []
            else:
                xv = xmid[l - 1].ap().rearrange("(t p) d -> t p d", p=128)

                def xsrc(tt, xv=xv, l=l):
                    return xv[tt], [f"xmid{l - 1}_{tt}"]
            prm = dict(wu=wu_d[l], wff=wff_d[l], gpre=gpre_d[l], fb=fb_d[l], lam=lam_d[l], subg=subg_d[l])

            def ystore(u, qb, l=l):
                return ysrc[l][u].ap()[:, qb * 512:(qb + 1) * 512], [f"ysrc{u}_{qb}"]

            def unit_done(u, l=l):
                P.cc(lambda E: E.collective_compute("AllGather", ALU.bypass, replica_groups=PAIRS,
                                                    ins=[ysrc[l][u].ap().opt()], outs=[ydst[l][u].ap().opt()]),
                     reads=[f"ysrc{u}_{qb}" for qb in range(NTB)], writes=[f"ydst{u}"])
            def pre_out(l=l):
                emit_outproj(P, B, S, None, None, wo_d[l], gp_d[l], None, phase="prefetch")
            emit_attn(P, B, S, lam_init, UNITS, xsrc, prm, rope_d, ystore, unit_done, after_last_proj=pre_out)
            if l < depth - 1:
                ov = xmid[l].ap().rearrange("(t p) d -> t p d", p=128)

                def ostore(tt, ov=ov, l=l):
                    return ov[tt], [f"xmid{l}_{tt}"]
            else:
                ov = o_d.rearrange("(t p) d -> t p d", p=128)

                def ostore(tt, ov=ov):
                    return ov[tt], []
            emit_outproj(P, B, S, [t.ap() for t in ydst[l]], xsrc, wo_d[l], gp_d[l], ostore, phase="main")
        P.emit(nc)
        nc._prog_stats = P.stats
    return nc


def _unit_cols(hh):
    cols = []
    for t in range(3):
        base = t * 2048
        for uu in range(2):
            j = 2 * hh + uu
            cols.append([base + g * 512 + j * 128 for g in range(4)])
    ycols = []
    for t in range(3):
        for uu in range(2):
            j = 2 * hh + uu
            ycols.append(t * 512 + j * 128)
    return cols, ycols


def _attn_inputs(x_b, w_in_l, fb_l, lam_l, subg_l, gpre_l, hh, consts):
    cb, cf, rope = consts
    cols, _ = _unit_cols(hh)
    wu = np.stack([np.concatenate([w_in_l[:, c:c + 128] for c in cc], axis=1) for cc in cols], axis=0)
    ffc = 6144 + 4 * hh
    wff = np.ascontiguousarray(w_in_l[:, ffc:ffc + 4].reshape(KC, 128, 4).transpose(1, 0, 2))
    return {
        "x": np.ascontiguousarray(x_b), "wu": np.ascontiguousarray(wu), "wff": wff,
        "gpre": np.ascontiguousarray(gpre_l.reshape(KC, 128).T),
        "fb": np.ascontiguousarray(fb_l[4 * hh:4 * hh + 4].reshape(4, 1)),
        "lam": np.ascontiguousarray(np.broadcast_to(lam_l.reshape(1, 256), (128, 256))),
        "subg": np.ascontiguousarray(subg_l.reshape(128, 1)),
        "cb": cb, "cf": cf, "rope": rope,
    }


def _wo_perm():
    rows = []
    for u in range(6):
        for r in range(2):
            _, ycols = _unit_cols(r)
            rows.append(ycols[u])
    return rows


def _fused_inputs(x_b, w_in, forget_bias, diff_lambda, diff_subln, w_out, pre_norm, post_norm, hh, consts):
    cb, cf, rope = consts
    depth = w_in.shape[0]
    per = [_attn_inputs(x_b, w_in[l], forget_bias[l], diff_lambda[l], diff_subln[l], pre_norm[l], hh, consts)
           for l in range(depth)]
    rows = _wo_perm()
    wo = np.stack([np.stack([w_out[l][r:r + 128] for r in rows], axis=0) for l in range(depth)], axis=0)
    gp = np.stack([np.broadcast_to(post_norm[l].reshape(1, D_MODEL), (128, D_MODEL)) for l in range(depth)], axis=0)
    m = {"x": np.ascontiguousarray(x_b), "cb": cb, "cf": cf, "rope": rope,
         "wo": np.ascontiguousarray(wo, dtype=np.float32), "gpost": np.ascontiguousarray(gp, dtype=np.float32)}
    for k in ("wu", "wff", "gpre", "fb", "lam", "subg"):
        m[k] = np.ascontiguousarray(np.stack([p[k] for p in per], axis=0))
    return m


def kernel(x, w_in, forget_bias, diff_lambda, diff_subln, w_out, pre_norm, post_norm):
    x = np.asarray(x, np.float32)
    w_in, forget_bias, diff_lambda, diff_subln, w_out, pre_norm, post_norm = [
        np.asarray(a, np.float32) for a in (w_in, forget_bias, diff_lambda, diff_subln, w_out, pre_norm, post_norm)]
    Bsz, S, D = x.shape
    consts = _consts(S)
    nc = build_fused(S)
    in_maps = [_fused_inputs(x[c // 2], w_in, forget_bias, diff_lambda, diff_subln, w_out, pre_norm, post_norm, c % 2,
                             consts) for c in range(8)]
    res = run_bass_kernel_spmd(nc, in_maps, core_ids=list(range(8)))
    out = np.empty_like(x)
    half = S // 2
    for c in range(8):
        b, hh = c // 2, c % 2
        out[b, hh * half:(hh + 1) * half] = res.results[c]["xo"][hh * half:(hh + 1) * half]
    return out
```

```python
import contextlib
import math
import numpy as np
import ml_dtypes
import concourse.bass as bass
import concourse.mybir as mybir
from concourse.bass_utils import run_bass_kernel_spmd

F32 = mybir.dt.float32
BF16 = mybir.dt.bfloat16
AF = mybir.ActivationFunctionType
ALU = mybir.AluOpType

D_MODEL = 1024
DEPTH = 2
NEG = -30000.0
EPS = 1e-6
ROPE_THETA = 500000.0
KC = 8

CB_IDENT, CB_NEGTRI, CB_NEGONES, CB_MASKA, CB_MASKB, CB_MASKC, CB_ONES, NCB = 0, 128, 256, 384, 1280, 2176, 3072, 3200
CF_MEAN, CF_ONES, CF_SEL, CF_IDENT4, NCF = 0, 128, 256, 768, 772

COMPUTE = ("pe", "act", "dve", "pool")


class _Op:
    __slots__ = ("eng", "fn", "dma", "sig", "deps", "dprev", "cc")

    def __init__(self, eng, fn, dma, cc=False):
        self.eng = eng
        self.fn = fn
        self.dma = dma
        self.cc = cc
        self.sig = None
        self.deps = {}
        self.dprev = None


class Prog:
    def __init__(self, n_dma_sems=16):
        self.ops = []
        self.last_writer = {}
        self.readers = {}
        self.n_dma_sems = n_dma_sems

    def op(self, eng, fn, reads=(), writes=(), dma=False, cc=False):
        o = _Op(eng, fn, dma, cc)
        for k in reads:
            w = self.last_writer.get(k)
            if w is not None:
                o.deps[w] = "raw"
        for k in writes:
            w = self.last_writer.get(k)
            if w is not None and w not in o.deps:
                o.deps[w] = "waw"
            for r in self.readers.get(k, ()):
                if r not in o.deps:
                    o.deps[r] = "war"
        for k in reads:
            self.readers.setdefault(k, []).append(o)
        for k in writes:
            self.last_writer[k] = o
            self.readers[k] = []
        self.ops.append(o)
        return o

    def pe(self, fn, reads=(), writes=()):
        return self.op("pe", fn, reads, writes)

    def act(self, fn, reads=(), writes=()):
        return self.op("act", fn, reads, writes)

    def dve(self, fn, reads=(), writes=()):
        return self.op("dve", fn, reads, writes)

    def pool(self, fn, reads=(), writes=()):
        return self.op("pool", fn, reads, writes)

    def dma(self, q, fn, reads=(), writes=()):
        return self.op(q, fn, reads, writes, dma=True)

    def cc(self, fn, reads=(), writes=()):
        return self.op("pool", fn, reads, writes, dma=True, cc=True)

    def emit(self, nc, final_wait_engine="sp"):
        ops = self.ops
        need = {}
        signalled = set()
        for o in ops:
            lst = []
            for d, kind in o.deps.items():
                if d.dma or o.dma:
                    lst.append(d)
                elif d.eng == o.eng:
                    if o.eng == "pe" or kind != "raw":
                        continue
                    lst.append(d)
                else:
                    lst.append(d)
            need[o] = lst
            signalled.update(lst)
        cnt = {e: 0 for e in COMPUTE}
        dcnt, duse = {}, {}
        ncc = 0
        for o in ops:
            if o.cc:
                ncc += 1
                o.sig = (("cc",), ncc)
            elif o.dma:
                i = dcnt.get(o.eng, 0)
                dcnt[o.eng] = i + 1
                key = ("dma", o.eng, i % self.n_dma_sems)
                u = duse.get(key, 0) + 1
                duse[key] = u
                o.sig = (key, 16 * u)
                o.dprev = (key, 16 * (u - 1)) if u > 1 else None
            elif o in signalled:
                cnt[o.eng] += 1
                o.sig = (o.eng, cnt[o.eng])
        semkeys = sorted(set(o.sig[0] for o in ops if o.sig is not None), key=str)
        self.stats = dict(n_ops=len(ops), sig=dict(cnt), n_sems=len(semkeys))
        with contextlib.ExitStack() as es:
            sems = {}
            for k in semkeys:
                nm = "s_" + "_".join(str(x) for x in (k if isinstance(k, tuple) else (k,)))
                sems[k] = es.enter_context(nc.semaphore(nm))
            block = es.enter_context(nc.Block())
            dmas = [o for o in ops if o.dma]

            def run(engname, E):
                known = {}
                nwait = 0
                for o in ops:
                    if o.eng != engname:
                        continue
                    ws = {}
                    for d in need[o]:
                        k, v = d.sig
                        if ws.get(k, 0) < v:
                            ws[k] = v
                    if o.dma and o.dprev is not None:
                        k, v = o.dprev
                        if ws.get(k, 0) < v:
                            ws[k] = v
                    for k, v in ws.items():
                        if known.get(k, 0) >= v:
                            continue
                        E.wait_ge(sems[k], v)
                        known[k] = v
                        nwait += 1
                    ins = o.fn(E)
                    if o.sig is not None:
                        if o.cc:
                            ins.then_inc(sems[o.sig[0]])
                        else:
                            ins.then_inc(sems[o.sig[0]], 16 if o.dma else 1)
                if engname == final_wait_engine:
                    last = {}
                    for o in dmas:
                        k, v = o.sig
                        last[k] = max(last.get(k, 0), v)
                    for k, v in last.items():
                        if known.get(k, 0) < v:
                            E.wait_ge(sems[k], v)
                self.stats["waits_" + engname] = nwait

            block.tensor(lambda E: run("pe", E))
            block.scalar(lambda E: run("act", E))
            block.vector(lambda E: run("dve", E))
            block.gpsimd(lambda E: run("pool", E))
            block.sync(lambda E: run("sp", E))


def _consts(S):
    cb = np.zeros((128, NCB), np.float32)
    cb[:, CB_IDENT:CB_IDENT + 128] = np.eye(128)
    j = np.arange(128)[:, None]
    s = np.arange(128)[None, :]
    cb[:, CB_NEGTRI:CB_NEGTRI + 128] = np.where(j >= s, -1.0, 0.0)
    cb[:, CB_NEGONES:CB_NEGONES + 128] = -1.0
    cb[:, CB_ONES:CB_ONES + 128] = 1.0
    k = np.arange(128)[:, None]
    qrel = np.arange(896)[None, :] - 384
    in_diag = (qrel >= 0) & (qrel < 128)
    okA = np.where(qrel < 0, False, np.where(in_diag, (k // 64) <= (qrel // 64), True))
    okB = k < qrel
    okC = k <= qrel
    cb[:, CB_MASKA:CB_MASKA + 896] = np.where(okA, 0.0, NEG)
    cb[:, CB_MASKB:CB_MASKB + 896] = np.where(okB, 0.0, NEG)
    cb[:, CB_MASKC:CB_MASKC + 896] = np.where(okC, 0.0, NEG)
    cb = cb.astype(ml_dtypes.bfloat16)
    cf = np.zeros((128, NCF), np.float32)
    cf[:, CF_MEAN:CF_MEAN + 128] = 1.0 / 128.0
    cf[:, CF_ONES:CF_ONES + 128] = 1.0
    for h in range(4):
        cf[h, CF_SEL + h * 128:CF_SEL + (h + 1) * 128] = 1.0
        cf[h, CF_IDENT4 + h] = 1.0
    pos = np.arange(S, dtype=np.float32)
    inv_freq = (np.float32(ROPE_THETA) ** (-np.arange(0, 16, 2, dtype=np.float32) / np.float32(16))).astype(np.float32)
    ang = (pos[:, None] * inv_freq[None, :]).astype(np.float32)
    cos, sin = np.cos(ang).astype(np.float32), np.sin(ang).astype(np.float32)
    rope = np.zeros((2, 128, S), np.float32)
    rope[0] = 1.0
    for p in range(128):
        d = p % 64
        if d < 16:
            rope[0, p] = cos[:, d % 8]
            rope[1, p] = sin[:, d % 8]
    return cb, cf, rope


ARENA = KC * 4096


def alloc_bufs(nc, es, S):
    from types import SimpleNamespace
    NTB, NTT = S // 512, S // 128

    def sb(name, shape, dt):
        return es.enter_context(nc.sbuf_tensor("sb_" + name, shape, dt))
    B = SimpleNamespace()
    B.arena = sb("arena", [128, max(ARENA, KC * S)], BF16)
    B.hT = B.arena[:, 0:KC * S].rearrange("p (k s) -> p k s", k=KC)
    B.cb = sb("cb", [128, NCB], BF16)
    B.cf = sb("cf", [128, NCF], F32)
    B.stage = sb("stage", [128, 2, 1024], F32)
    B.Wb = sb("Wb", [128, KC, 512], BF16)
    B.Wrot = sb("Wrot", [128, KC, 256], BF16)
    B.Wffb = sb("Wffb", [128, KC, 4], BF16)
    B.wff_f = sb("wff_f", [128, KC, 4], F32)
    B.QT = sb("QT", [128, S], BF16)
    B.KTz = [sb(f"KT{i}z", [128, S], BF16) for i in range(2)]
    B.GT = sb("GT", [128, S], BF16)
    B.VA = sb("VA", [128, NTT, 128], BF16)
    B.VB = sb("VB", [128, NTT, 128], BF16)
    B.ropeb = [sb(f"ropeb{i}", [128, 2, 512], F32) for i in range(2)]
    B.gpre = sb("gpre", [128, KC], F32)
    B.rstd = sb("rstd", [128, NTT], F32)
    B.ssq = sb("ssq", [128, NTT], F32)
    B.small = sb("small", [128, 16], F32)
    B.lamt = sb("lamt", [128, 256], F32)
    B.lamp = sb("lamp", [128, 128], F32)
    B.subg = sb("subg", [128, 1], F32)
    B.fbt = sb("fbt", [4, 2], F32)
    B.cT = sb("cT", [4, S], F32)
    B.negck = sb("negck", [128, NTT, 4], F32)
    B.e_sb = [sb(f"e{i}", [128, 512], F32) for i in range(2)]
    B.sp2 = [sb(f"sp2_{i}", [128, 1024], BF16) for i in range(2)]
    B.sp_sb = [B.sp2[i][:, 0:512] for i in range(2)]
    B.pT2 = [sb(f"pT2_{i}", [128, 1024], BF16) for i in range(2)]
    B.pT_sb = [B.pT2[i // 2][:, (i % 2) * 512:(i % 2 + 1) * 512] for i in range(4)]
    B.ss2 = [sb(f"ss2_{i}", [128, 1024], F32) for i in range(2)]
    B.ss_sb = [B.ss2[i // 2][:, (i % 2) * 512:(i % 2 + 1) * 512] for i in range(4)]
    B.e2 = [B.e_sb[i][:].bitcast(BF16) for i in range(2)]
    B.xh = [B.e_sb[i][:].bitcast(BF16) for i in range(2)]
    B.sqj = B.ss_sb[0].bitcast(BF16)
    B.cqb_sb = [sb(f"cqb{i}", [128, 512], F32) for i in range(2)]
    B.tA = sb("tA", [128, 512], F32)
    B.tB = sb("tB", [128, 512], F32)
    B.tC = sb("tC", [128, 512], F32)
    B.fxe, B.fxs, B.ones4 = B.tA[0:4, :], B.tB[0:4, :], B.tC[0:4, :]
    B.drun = [sb(f"drun{i}", [128, 512], F32) for i in range(2)]
    B.y_sb = [sb(f"y{i}", [128, 512], BF16) for i in range(2)]
    B.st = sb("st", [128, 8], F32)
    B.ps2 = [es.enter_context(nc.psum_tensor(f"ps2_{i}", [128, 1024], F32)) for i in range(4)]
    B.ps = [B.ps2[i // 2][:, (i % 2) * 512:(i % 2 + 1) * 512] for i in range(8)]
    B.pst = [B.ps[6 + i].bitcast(BF16) for i in range(2)]
    B.ctr = {"rope": 0, "O": 0, "y": 0, "pT": 0, "S": 0, "e": 0, "cqb": 0, "ss": 0, "R": 0}
    return B


def emit_consts(P, B, cb_d, cf_d):
    P.dma("sp", lambda E: E.dma_start(out=B.cb[:], in_=cb_d), writes=["cb"])
    P.dma("sp", lambda E: E.dma_start(out=B.cf[:], in_=cf_d), writes=["cf"])
    P.pool(lambda E: E.memset(B.KTz[0][64:128, :], 0.0), writes=["KTzero"])
    P.pool(lambda E: E.memset(B.KTz[1][0:64, :], 0.0), writes=["KTzero"])


def build_attn(S, lambda_init, units=("A", "A", "B", "B", "C", "C")):
    nc = bass.Bass("TRN2", target_bir_lowering=False)
    x_d = nc.dram_tensor("x", [S, D_MODEL], F32, kind="ExternalInput").ap()
    prm = dict(
        wu=nc.dram_tensor("wu", [6, D_MODEL, 512], F32, kind="ExternalInput").ap(),
        wff=nc.dram_tensor("wff", [128, KC, 4], F32, kind="ExternalInput").ap(),
        gpre=nc.dram_tensor("gpre", [128, KC], F32, kind="ExternalInput").ap(),
        fb=nc.dram_tensor("fb", [4, 1], F32, kind="ExternalInput").ap(),
        lam=nc.dram_tensor("lam", [128, 256], F32, kind="ExternalInput").ap(),
        subg=nc.dram_tensor("subg", [128, 1], F32, kind="ExternalInput").ap())
    cb_d = nc.dram_tensor("cb", [128, NCB], BF16, kind="ExternalInput").ap()
    cf_d = nc.dram_tensor("cf", [128, NCF], F32, kind="ExternalInput").ap()
    rope_d = nc.dram_tensor("rope", [2, 128, S], F32, kind="ExternalInput").ap()
    yT_d = nc.dram_tensor("yT", [768, S], BF16, kind="ExternalOutput").ap()
    xv = x_d.rearrange("(t p) d -> t p d", p=128)
    with contextlib.ExitStack() as es:
        B = alloc_bufs(nc, es, S)
        P = Prog()
        emit_consts(P, B, cb_d, cf_d)
        emit_attn(P, B, S, lambda_init, units, lambda tt: (xv[tt], []), prm, rope_d,
                  lambda u, qb: (yT_d[u * 128:(u + 1) * 128, qb * 512:(qb + 1) * 512], []), lambda u: None)
        P.emit(nc)
        nc._prog_stats = P.stats
    return nc


def emit_attn(P, B, S, lambda_init, units, xsrc, prm, rope_d, ystore, unit_done, after_last_proj=None):
    NTB, NTT = S // 512, S // 128
    hT, cb, cf, stage, Wb, Wrot, Wffb, wff_f = B.hT, B.cb, B.cf, B.stage, B.Wb, B.Wrot, B.Wffb, B.wff_f
    QT, KTz, GT, VA, VB, ropeb, gpre, rstd, ssq, sqj, xh = B.QT, B.KTz, B.GT, B.VA, B.VB, B.ropeb, B.gpre, B.rstd, B.ssq, B.sqj, B.xh
    small, lamt, lamp, subg, fbt, cT, fxe, fxs, ones4, negck = B.small, B.lamt, B.lamp, B.subg, B.fbt, B.cT, B.fxe, B.fxs, B.ones4, B.negck
    e_sb, sp_sb, pT_sb, ss_sb, cqb_sb, tA, tB, tC, drun, y_sb = B.e_sb, B.sp_sb, B.pT_sb, B.ss_sb, B.cqb_sb, B.tA, B.tB, B.tC, B.drun, B.y_sb
    ps, pst = B.ps, B.pst
    PS = [f"ps{i}" for i in range(8)]
    wu_d, wff_d, gpre_d, fb_d, lam_d, subg_d = prm["wu"], prm["wff"], prm["gpre"], prm["fb"], prm["lam"], prm["subg"]

    ident = cb[:, CB_IDENT:CB_IDENT + 128]
    negtri = cb[:, CB_NEGTRI:CB_NEGTRI + 128]
    negones = cb[:, CB_NEGONES:CB_NEGONES + 128]
    maskoff = {"A": CB_MASKA, "B": CB_MASKB, "C": CB_MASKC}

    P.dma("sp", lambda E: E.dma_start(out=gpre[:], in_=gpre_d), writes=["gpre"])
    P.dma("sp", lambda E: E.dma_start(out=lamt[:], in_=lam_d), writes=["lamt"])
    P.dma("sp", lambda E: E.dma_start(out=subg[:], in_=subg_d), writes=["subg"])
    P.dma("sp", lambda E: E.dma_start(out=fbt[:, 0:1], in_=fb_d), writes=["fbt"])
    P.dma("sp", lambda E: E.dma_start(out=wff_f[:], in_=wff_d), writes=["wff_f"])
    P.dve(lambda E: E.tensor_tensor(out=lamp[:, 0:64], in0=lamt[:, 0:64], in1=lamt[:, 64:128], op=ALU.mult),
          reads=["lamt"], writes=["lamp"])
    P.dve(lambda E: E.tensor_tensor(out=lamp[:, 64:128], in0=lamt[:, 128:192], in1=lamt[:, 192:256], op=ALU.mult),
          reads=["lamt", "lamp"], writes=["lamp"])
    P.dve(lambda E: E.reduce_sum(out=small[:, 0:2], in_=lamp[:].rearrange("p (a b) -> p a b", a=2),
                                 axis=mybir.AxisListType.X), reads=["lamp"], writes=["small"])
    P.act(lambda E: E.activation(out=small[:, 2:4], in_=small[:, 0:2], func=AF.Exp), reads=["small"], writes=["small"])
    P.dve(lambda E: E.tensor_tensor(out=small[:, 4:5], in0=small[:, 3:4], in1=small[:, 2:3], op=ALU.subtract),
          reads=["small"], writes=["small"])
    P.dve(lambda E: E.tensor_scalar(out=small[:, 4:5], in0=small[:, 4:5], scalar1=-float(lambda_init), scalar2=None,
                                    op0=ALU.add), reads=["small"], writes=["small"])
    P.dve(lambda E: E.tensor_scalar(out=small[:, 5:6], in0=subg[:, 0:1], scalar1=float(1.0 - lambda_init), scalar2=None,
                                    op0=ALU.mult), reads=["small", "subg"], writes=["small"])
    neglam = small[:, 4:5]
    sgl = small[:, 5:6]
    P.dve(lambda E: E.tensor_scalar(out=fbt[:, 1:2], in0=fbt[:, 0:1], scalar1=-1.0, scalar2=None, op0=ALU.mult),
          reads=["fbt"], writes=["fbt"])
    P.dve(lambda E: E.tensor_tensor(out=Wffb[:], in0=wff_f[:], in1=gpre[:].unsqueeze(2).broadcast_to([128, KC, 4]),
                                    op=ALU.mult), reads=["wff_f", "gpre"], writes=["Wffb"])

    wv = wu_d.rearrange("u (kc p) c -> u p kc c", p=128)
    ctr = B.ctr
    _first = [u for u, t in enumerate(units) if t is not None][0]
    def load_weights(u, utype):
        for qtr in range(4):
            b = qtr % 2
            P.dma("sp", lambda E, qtr=qtr, b=b: E.dma_start(
                out=stage[:, b, :].rearrange("p (k c) -> p k c", k=2), in_=wv[u][:, qtr * 2:(qtr + 1) * 2, :]),
                writes=[f"stage{b}"])
            for k2 in range(2):
                kc = qtr * 2 + k2
                P.pool(lambda E, b=b, k2=k2, kc=kc: E.tensor_scalar(
                    out=Wb[:, kc, 0:128], in0=stage[:, b, k2 * 512:k2 * 512 + 128], scalar1=gpre[:, kc:kc + 1],
                    scalar2=0.125, op0=ALU.mult, op1=ALU.mult), reads=[f"stage{b}", "gpre"], writes=["Wb"])
                P.pool(lambda E, b=b, k2=k2, kc=kc: E.tensor_scalar(
                    out=Wb[:, kc, 128:512], in0=stage[:, b, k2 * 512 + 128:(k2 + 1) * 512],
                    scalar1=gpre[:, kc:kc + 1], scalar2=1.0, op0=ALU.mult, op1=ALU.mult), reads=[f"stage{b}", "gpre"],
                    writes=["Wb"])
        if utype == "A":
            P.pool(lambda E: E.memset(Wrot[:], 0.0), reads=[], writes=["Wrot"])
            for g in range(2):
                for m in range(2):
                    src0 = g * 128 + m * 64
                    dst0 = g * 128 + m * 64
                    P.pool(lambda E, s=src0, d=dst0: E.tensor_scalar(out=Wrot[:, :, d:d + 8], in0=Wb[:, :, s + 8:s + 16],
                                                                     scalar1=-1.0, scalar2=1.0, op0=ALU.mult,
                                                                     op1=ALU.mult),
                           reads=["Wb", "Wrot"], writes=["Wrot"])
                    P.pool(lambda E, s=src0, d=dst0: E.tensor_copy(out=Wrot[:, :, d + 8:d + 16], in_=Wb[:, :, s:s + 8]),
                           reads=["Wb", "Wrot"], writes=["Wrot"])

    load_weights(_first, units[_first])

    P.pool(lambda E: E.memset(small[:, 8:9], -0.5), writes=["small8"])

    def p1_load(tt):
        b = tt % 2
        xap, xkeys = xsrc(tt)
        P.dma("sp", lambda E: E.dma_start(out=stage[:, b, :], in_=xap), reads=xkeys, writes=[f"stage{b}"])
        P.act(lambda E: E.activation(out=sqj, in_=stage[:, b, :], func=AF.Square, accum_out=ssq[:, tt:tt + 1]),
              reads=[f"stage{b}"], writes=["ss0", f"ssq{b}"])
        P.pool(lambda E: E.tensor_scalar(out=rstd[:, tt:tt + 1], in0=ssq[:, tt:tt + 1], scalar1=1.0 / D_MODEL,
                                         scalar2=EPS, op0=ALU.mult, op1=ALU.add), reads=[f"ssq{b}"],
               writes=[f"rstd{b}"])
        P.pool(lambda E: E.tensor_tensor(out=rstd[:, tt:tt + 1], in0=rstd[:, tt:tt + 1], in1=small[:, 8:9],
                                         op=ALU.pow), reads=[f"rstd{b}", "small8"], writes=[f"rstd{b}"])

    def p1_scale(tt):
        b = tt % 2
        P.dve(lambda E: E.tensor_scalar(out=xh[b], in0=stage[:, b, :], scalar1=rstd[:, tt:tt + 1], scalar2=None,
                                        op0=ALU.mult), reads=[f"stage{b}", f"rstd{b}"], writes=[f"e{b}"])
        for kc in range(KC):
            P.pe(lambda E, kc=kc: E.transpose(out=pst[b][:, kc * 128:(kc + 1) * 128],
                                              in_=xh[b][:, kc * 128:(kc + 1) * 128], identity=ident),
                 reads=[f"e{b}", "cb"], writes=[f"ps{6 + b}"])

    def p1_evac(tt):
        b = tt % 2
        P.dve(lambda E: E.tensor_copy(out=hT[:, :, tt * 128:(tt + 1) * 128],
                                      in_=pst[b].rearrange("p (k t) -> p k t", k=KC)),
              reads=[f"ps{6 + b}"], writes=[f"hT{tt // 4}", "arena"])

    for step in range(NTT + 2):
        if step < NTT:
            p1_load(step)
        if 1 <= step <= NTT:
            p1_scale(step - 1)
        if step >= 2:
            p1_evac(step - 2)
    HT = [f"hT{i}" for i in range(NTB)]

    if "C" in units:
        P.pool(lambda E: E.memset(ones4, 1.0), writes=["tC"])
        for tb in range(NTB):
            pb = 4 + tb % 2
            for kc in range(KC):
                P.pe(lambda E, tb=tb, kc=kc, pb=pb: E.matmul(ps[pb][0:4, :], lhsT=Wffb[:, kc, :],
                                                           rhs=hT[:, kc, tb * 512:(tb + 1) * 512],
                                                           start=(kc == 0), stop=(kc == KC - 1)),
                     reads=["Wffb", HT[tb]], writes=[PS[pb]])
            P.act(lambda E, pb=pb: E.activation(out=fxe, in_=ps[pb][0:4, :], func=AF.Exp, bias=fbt[:, 1:2],
                                                scale=-1.0), reads=[PS[pb], "fbt"], writes=["tA"])
            P.act(lambda E: E.activation(out=fxs, in_=fxe, func=AF.Ln, bias=1.0, scale=1.0),
                  reads=["tA"], writes=["tB"])
            init = 0.0 if tb == 0 else cT[:, tb * 512 - 1:tb * 512]
            P.dve(lambda E, tb=tb, init=init: E.tensor_tensor_scan(out=cT[:, tb * 512:(tb + 1) * 512], data0=ones4,
                                                                   data1=fxs, initial=init, op0=ALU.mult,
                                                                   op1=ALU.subtract),
                  reads=["tB", "tC", "cT"], writes=["cT"])
        for kb in range(NTT):
            P.pe(lambda E, kb=kb: E.transpose(out=ps[5][:, kb * 4:(kb + 1) * 4], in_=cT[:, kb * 128:(kb + 1) * 128],
                                              identity=cf[0:4, CF_IDENT4:CF_IDENT4 + 4]),
                 reads=["cT", "cf"], writes=[PS[5]])
        P.dve(lambda E: E.tensor_scalar(out=negck[:].rearrange("p a b -> p (a b)"), in0=ps[5][:, 0:NTT * 4],
                                        scalar1=-1.0, scalar2=None, op0=ALU.mult), reads=[PS[5]], writes=["negck"])

    ones_done = [False]
    zeros_done = [False]

    def proj_group(wt, c0, tb, pb, wkey):
        for kc in range(KC):
            P.pe(lambda E, kc=kc: E.matmul(ps[pb][:], lhsT=wt[:, kc, c0:c0 + 128], rhs=hT[:, kc, tb * 512:(tb + 1) * 512],
                                           start=(kc == 0), stop=(kc == KC - 1)),
                 reads=[wkey, HT[tb]], writes=[PS[pb]])

    def projections(utype):
        for tb in range(NTB):
            sl = slice(tb * 512, (tb + 1) * 512)
            if utype == "A":
                rb = ctr["rope"] % 2
                ctr["rope"] += 1
                P.dma("sp", lambda E, rb=rb, sl=sl: E.dma_start(out=ropeb[rb][:], in_=rope_d[:, :, sl].rearrange("a p s -> p a s")),
                      writes=[f"ropeb{rb}"])
                for g, dkey in enumerate(("QT", "KT")):
                    proj_group(Wb, g * 128, tb, 0, "Wb")
                    proj_group(Wrot, g * 128, tb, 1, "Wrot")
                    P.dve(lambda E, rb=rb: E.tensor_tensor(out=tA[:], in0=ps[0][:], in1=ropeb[rb][:, 0, :], op=ALU.mult),
                          reads=[PS[0], f"ropeb{rb}"], writes=["tA"])
                    P.dve(lambda E, rb=rb: E.tensor_tensor(out=tB[:], in0=ps[1][:], in1=ropeb[rb][:, 1, :], op=ALU.mult),
                          reads=[PS[1], f"ropeb{rb}"], writes=["tB"])
                    if g == 0:
                        P.pool(lambda E, sl=sl: E.tensor_tensor(out=QT[:, sl], in0=tA[:], in1=tB[:], op=ALU.add),
                               reads=["tA", "tB"], writes=[f"QT{tb}"])
                    else:
                        for mm in range(2):
                            P.pool(lambda E, sl=sl, mm=mm: E.tensor_tensor(
                                out=KTz[mm][mm * 64:(mm + 1) * 64, sl], in0=tA[mm * 64:(mm + 1) * 64, :],
                                in1=tB[mm * 64:(mm + 1) * 64, :], op=ALU.add),
                                reads=["tA", "tB", "KTzero"], writes=[f"KT{tb}"])
            else:
                proj_group(Wb, 0, tb, 0, "Wb")
                P.act(lambda E, sl=sl: E.activation(out=QT[:, sl], in_=ps[0][:], func=AF.Copy), reads=[PS[0]],
                      writes=[f"QT{tb}"])
                proj_group(Wb, 128, tb, 1, "Wb")
                for mm in range(2):
                    P.dve(lambda E, sl=sl, mm=mm: E.tensor_copy(out=KTz[mm][mm * 64:(mm + 1) * 64, sl],
                                                               in_=ps[1][mm * 64:(mm + 1) * 64, :]),
                          reads=[PS[1], "KTzero"], writes=[f"KT{tb}"])
            proj_group(Wb, 384, tb, 2, "Wb")
            P.act(lambda E, sl=sl: E.activation(out=GT[:, sl], in_=ps[2][:], func=AF.Silu), reads=[PS[2]],
                  writes=[f"GT{tb}"])
            for t4 in range(4):
                tt = tb * 4 + t4
                for kc in range(KC):
                    P.pe(lambda E, tt=tt, t4=t4, kc=kc: E.matmul(ps[3][:, t4 * 128:(t4 + 1) * 128],
                                                                lhsT=hT[:, kc, tt * 128:(tt + 1) * 128],
                                                                rhs=Wb[:, kc, 256:384], start=(kc == 0),
                                                                stop=(kc == KC - 1)),
                         reads=["Wb", HT[tb]], writes=[PS[3]])
            pv = ps[3][:].rearrange("p (t c) -> p t c", t=4)
            if utype in ("B", "C"):
                d1 = 0 if utype == "C" else 64
                P.dve(lambda E, tb=tb, pv=pv: E.tensor_copy(out=VA[:, tb * 4:(tb + 1) * 4, 0:64], in_=pv[:, :, 0:64]),
                      reads=[PS[3]], writes=[f"VA{tb}"])
                P.dve(lambda E, tb=tb, pv=pv, d1=d1: E.tensor_copy(out=VB[:, tb * 4:(tb + 1) * 4, d1:d1 + 64],
                                                                 in_=pv[:, :, 64:128]),
                      reads=[PS[3]], writes=[f"VB{tb}"])
            else:
                P.dve(lambda E, tb=tb, pv=pv: E.tensor_copy(out=VA[:, tb * 4:(tb + 1) * 4, :], in_=pv),
                      reads=[PS[3]], writes=[f"VA{tb}"])

    def qk(utype, m, kb, qb, sbank):
        j = kb - 4 * qb
        diag = j >= 0
        c0 = 128 * j if j > 0 else 0
        P.pe(lambda E: E.matmul(ps[sbank][:, c0:512], lhsT=KTz[m][:, kb * 128:(kb + 1) * 128],
                                rhs=QT[:, qb * 512 + c0:(qb + 1) * 512], start=True, stop=True),
             reads=[f"KT{kb // 4}", f"QT{qb}"], writes=[PS[sbank]])
        if diag:
            mo = maskoff[utype] + 384
            P.pe(lambda E: E.matmul(ps[sbank][:, c0:c0 + 128], lhsT=ident, rhs=cb[:, mo:mo + 128], start=False, stop=True,
                                    skip_group_check=True), reads=["cb", PS[sbank]], writes=[PS[sbank]])
        return c0

    def out_dma(u, qb, yb):
        yap, ykeys = ystore(u, qb)
        P.dma("sp", lambda E: E.dma_start(out=yap, in_=y_sb[yb][:]), reads=[f"y{yb}"], writes=ykeys)

    pending = []
    cur_step = [0, 0]

    def defer(k, fn):
        cur_step[1] += 1
        pending.append((cur_step[0] + k, cur_step[1], fn))

    def run_pipeline(items, stages, lags):
        n, L = len(items), max(lags)
        del pending[:]
        step = 0
        while step < n + L or pending:
            cur_step[0] = step
            for f, lag in zip(stages, lags):
                i = step - lag
                if 0 <= i < n:
                    f(items[i])
            due = sorted([p for p in pending if p[0] <= step])
            for p in due:
                pending.remove(p)
            for p in due:
                p[2]()
            step += 1

    def make_items(order_desc=False):
        items = []
        for qb in range(NTB):
            nk = 4 * qb + 4
            for m in range(2):
                for i in range(nk):
                    items.append(dict(qb=qb, m=m, i=i, nk=nk, kb=(nk - 1 - i) if order_desc else i))
        return items

    ones_bf = cb[:, CB_ONES:CB_ONES + 128]

    def attn_A(u):
        items = make_items()
        grp = {}

        def st1(it):
            qb, m, i = it["qb"], it["m"], it["i"]
            if i == 0:
                grp[(qb, m)] = 2 + ctr["O"] % 2
                ctr["O"] += 1
            it["ob"] = grp[(qb, m)]
            sbank = (0, 1, 6, 7)[ctr["S"] % 4]
            ctr["S"] += 1
            it["sbank"] = sbank
            it["c0"] = qk("A", m, it["kb"], qb, sbank)

        def st2(it):
            sbank, c0 = it["sbank"], it["c0"]
            pb = ctr["pT"] % 4
            ctr["pT"] += 1
            it["pb"] = pb
            P.act(lambda E: E.activation(out=pT_sb[pb][:, c0:512], in_=ps[sbank][:, c0:512], func=AF.Exp),
                  reads=[PS[sbank]], writes=[f"pT{pb}"])

        def st3(it):
            qb, m, i, nk, kb, ob, pb = it["qb"], it["m"], it["i"], it["nk"], it["kb"], it["ob"], it["pb"]
            c0 = it["c0"]
            P.pe(lambda E: E.matmul(ps[ob][:, c0:512], lhsT=VA[:, kb, :], rhs=pT_sb[pb][:, c0:512], start=(i == 0),
                                    stop=(i == nk - 1)),
                 reads=[f"VA{kb // 4}", f"pT{pb}"], writes=[PS[ob]])
            g = it["g"]
            dset = (drun[0], drun[1]) if g % 2 == 0 else (cqb_sb[0], cqb_sb[1])
            dkeys = ("drun0", "drun1") if g % 2 == 0 else ("cqb0", "cqb1")
            if i % 4 == 3:
                P.pe(lambda E: E.matmul(ps[4][:, c0:512], lhsT=ones_bf, rhs=pT_sb[pb][:, c0:512], start=(i == 3),
                                        stop=False, skip_group_check=True),
                     reads=["cb", f"pT{pb}"], writes=[PS[4]])
                if i == nk - 1:
                    group_end_A(u, qb, m, ob, g, dset, dkeys)
                return
            par = (i - i // 4) % 2
            eng = P.dve
            dt_, dk_ = dset[par], dkeys[par]
            if i < 2:
                if c0 > 0:
                    eng(lambda E: E.memset(dt_[:, 0:c0], 0.0), writes=[dk_])
                eng(lambda E: E.tensor_copy(out=dt_[:, c0:512], in_=pT_sb[pb][:, c0:512]), reads=[f"pT{pb}"],
                    writes=[dk_])
            else:
                eng(lambda E: E.tensor_tensor(out=dt_[:, c0:512], in0=dt_[:, c0:512], in1=pT_sb[pb][:, c0:512],
                                              op=ALU.add),
                    reads=[f"pT{pb}", dk_], writes=[dk_])
            if i == nk - 1:
                group_end_A(u, qb, m, ob, g, dset, dkeys)

        def group_end_A(u, qb, m, ob, g, dset, dkeys):
            tD, tDk = e_sb[0], "e0"
            qsl = slice(qb * 512, (qb + 1) * 512)
            dsum, dsk = sp_sb[g % 2], f"sp{g % 2}"
            defer(0, lambda: P.dve(lambda E: E.tensor_tensor(out=dsum, in0=dset[0][:], in1=dset[1][:], op=ALU.add),
                                   reads=[dkeys[0], dkeys[1]], writes=[dsk]))
            defer(1, lambda: P.pe(lambda E: E.matmul(ps[4][:], lhsT=ones_bf, rhs=dsum, start=False, stop=True,
                                                     skip_group_check=True),
                                  reads=["cb", dsk, PS[4]], writes=[PS[4]]))
            defer(2, lambda: P.act(lambda E: E.activation(out=tA[:], in_=ps[4][:], func=AF.Ln), reads=[PS[4]],
                                   writes=["tA"]))
            defer(3, lambda: P.act(lambda E: E.activation(out=tA[:], in_=tA[:], func=AF.Exp, scale=-1.0),
                                   reads=["tA"], writes=["tA"]))
            if m == 0:
                defer(4, lambda: P.dve(lambda E: E.tensor_tensor(out=tB[:], in0=ps[ob][:], in1=tA[:], op=ALU.mult),
                                       reads=[PS[ob], "tA"], writes=["tB"]))
                return
            yb = ctr["y"] % 2
            ctr["y"] += 1
            defer(4, lambda: P.dve(lambda E: E.tensor_tensor(out=tC[:], in0=ps[ob][:], in1=tA[:], op=ALU.mult),
                                   reads=[PS[ob], "tA"], writes=["tC"]))
            defer(5, lambda: P.dve(lambda E: E.scalar_tensor_tensor(out=tB[:], in0=tC[:], scalar=neglam, in1=tB[:],
                                                                    op0=ALU.mult, op1=ALU.add),
                                   reads=["tB", "tC", "small"], writes=["tB"]))
            defer(6, lambda: P.pool(lambda E: E.tensor_tensor(out=tC[:], in0=tB[:], in1=tB[:], op=ALU.mult),
                                    reads=["tB"], writes=["tC"]))
            defer(7, lambda: P.pe(lambda E: E.matmul(ps[5][:], lhsT=cf[:, CF_MEAN:CF_MEAN + 128], rhs=tC[:], start=True,
                                                     stop=True), reads=["cf", "tC"], writes=[PS[5]]))
            defer(8, lambda: P.dve(lambda E: E.tensor_scalar(out=tD[:], in0=ps[5][:], scalar1=EPS, scalar2=None,
                                                             op0=ALU.add), reads=[PS[5]], writes=[tDk]))
            defer(9, lambda: P.act(lambda E: E.activation(out=tD[:], in_=tD[:], func=AF.Ln), reads=[tDk], writes=[tDk]))
            defer(10, lambda: P.act(lambda E: E.activation(out=tD[:], in_=tD[:], func=AF.Exp, scale=-0.5), reads=[tDk],
                                    writes=[tDk]))
            defer(11, lambda: P.dve(lambda E: E.scalar_tensor_tensor(out=tC[:], in0=tB[:], scalar=sgl, in1=tD[:],
                                                                     op0=ALU.mult, op1=ALU.mult),
                                    reads=["tB", tDk, "small"], writes=["tC"]))
            defer(12, lambda: P.dve(lambda E: E.tensor_tensor(out=y_sb[yb][:], in0=tC[:], in1=GT[:, qsl], op=ALU.mult),
                                    reads=["tC", f"GT{qb}"], writes=[f"y{yb}"]))
            defer(13, lambda: out_dma(u, qb, yb))

        for gi, it in enumerate(items):
            pass
        gidx = {}
        for it in items:
            key = (it["qb"], it["m"])
            if key not in gidx:
                gidx[key] = len(gidx)
            it["g"] = gidx[key]
        run_pipeline(items, [st1, st2, st3], [0, 2, 3])

    def attn_C(u, jpair):
        items = make_items()
        grp = {}
        SB3 = (0, 1, 5, 6)
        glist = []
        for it in items:
            key = (it["qb"], it["m"])
            if key not in glist:
                glist.append(key)

        def prep_group(gi):
            if gi >= len(glist) or glist[gi] in grp:
                return
            qb, m = glist[gi]
            h = jpair * 2 + m
            qsl = slice(qb * 512, (qb + 1) * 512)
            if m == 0:
                grp[("y", qb)] = ctr["y"] % 2
                ctr["y"] += 1
            ob = 2 + ctr["O"] % 2
            ctr["O"] += 1
            cq = ctr["cqb"] % 2
            ctr["cqb"] += 1
            grp[(qb, m)] = (ob, cq)
            P.pe(lambda E: E.matmul(ps[4][:], lhsT=cf[0:4, CF_SEL + h * 128:CF_SEL + (h + 1) * 128],
                                    rhs=cT[:, qsl], start=True, stop=True),
                 reads=["cf", "cT"], writes=[PS[4]])
            P.act(lambda E: E.activation(out=cqb_sb[cq][:], in_=ps[4][:], func=AF.Copy), reads=[PS[4]],
                  writes=[f"cqb{cq}"])

        def st1(it):
            qb, m, i = it["qb"], it["m"], it["i"]
            h = jpair * 2 + m
            gi = glist.index((qb, m))
            if i == 0:
                prep_group(gi)
            if i == 1 or (i == 0 and gi == 0):
                prep_group(gi + 1)
            ob, cq = grp[(qb, m)]
            it["ob"], it["h"], it["yb"] = ob, h, grp[("y", qb)]
            sbank = SB3[ctr["S"] % 4]
            ctr["S"] += 1
            c0 = qk("C", m, it["kb"], qb, sbank)
            it["c0"] = c0
            sidx = ctr["ss"] % 4
            ctr["ss"] += 1
            it["sidx"] = sidx
            kb = it["kb"]
            P.dve(lambda E: E.scalar_tensor_tensor(out=ss_sb[sidx][:, c0:512], in0=ps[sbank][:, c0:512],
                                                   scalar=negck[:, kb, h:h + 1], in1=cqb_sb[cq][:, c0:512],
                                                   op0=ALU.add, op1=ALU.add),
                  reads=[PS[sbank], f"cqb{cq}", "negck"], writes=[f"ss{sidx}"])

        def st2(it):
            sidx, c0 = it["sidx"], it["c0"]
            pb = ctr["pT"] % 4
            ctr["pT"] += 1
            it["pb"] = pb
            P.act(lambda E: E.activation(out=pT_sb[pb][:, c0:512], in_=ss_sb[sidx][:, c0:512], func=AF.Exp),
                  reads=[f"ss{sidx}"], writes=[f"pT{pb}"])

        def st3(it):
            qb, m, i, nk, kb, ob, pb, yb = it["qb"], it["m"], it["i"], it["nk"], it["kb"], it["ob"], it["pb"], it["yb"]
            Vh = VA if m == 0 else VB
            vkey = "VA" if m == 0 else "VB"
            c0 = it["c0"]
            P.pe(lambda E: E.matmul(ps[ob][:, c0:512], lhsT=Vh[:, kb, :], rhs=pT_sb[pb][:, c0:512], start=(i == 0),
                                    stop=(i == nk - 1)),
                 reads=[f"{vkey}{kb // 4}", f"pT{pb}"], writes=[PS[ob]])
            if i == nk - 1:
                group_end_C(qb, m, ob, yb)

        def group_end_C(qb, m, ob, yb):
            r0 = m * 64
            qsl = slice(qb * 512, (qb + 1) * 512)
            defer(1, lambda: P.act(lambda E: E.activation(out=tA[64:128, :], in_=ps[ob][64:128, :], func=AF.Ln),
                                   reads=[PS[ob]], writes=["tA"]))
            defer(2, lambda: P.act(lambda E: E.activation(out=tA[64:128, :], in_=tA[64:128, :], func=AF.Exp, scale=-1.0),
                                   reads=["tA"], writes=["tA"]))
            defer(3, lambda: P.dve(lambda E: E.tensor_tensor(out=tB[r0:r0 + 64, :], in0=ps[ob][0:64, :],
                                                             in1=tA[64:128, :], op=ALU.mult),
                                   reads=[PS[ob], "tA"], writes=["tB"]))
            defer(4, lambda: P.dve(lambda E: E.tensor_tensor(out=y_sb[yb][r0:r0 + 64, :], in0=tB[r0:r0 + 64, :],
                                                             in1=GT[r0:r0 + 64, qsl], op=ALU.mult),
                                   reads=["tB", f"GT{qb}"], writes=[f"y{yb}"]))
            if m == 1:
                defer(5, lambda: out_dma(u, qb, yb))

        run_pipeline(items, [st1, st2, st3], [0, 1, 2])

    def attn_B(u):
        sp2, pT2, ss2, e2, ps2 = B.sp2, B.pT2, B.ss2, B.e2, B.ps2
        items = []
        for qb in range(NTB):
            nk = 4 * qb + 4
            for m in range(2):
                for p in range(nk // 2):
                    items.append(dict(qb=qb, m=m, p=p, np=nk // 2, kba=nk - 1 - 2 * p, kbb=nk - 2 - 2 * p))
        grp = {}
        Ra, Rb = cqb_sb[0], cqb_sb[1]

        def act_pair(func, out2, in2, c0a, c0b, reads, writes, **kw):
            if c0a == 0 and c0b == 0:
                P.act(lambda E: E.activation(out=out2[:, 0:1024], in_=in2[:, 0:1024], func=func, **kw), reads=reads,
                      writes=writes)
            else:
                P.act(lambda E: E.activation(out=out2[:, c0a:512], in_=in2[:, c0a:512], func=func, **kw), reads=reads,
                      writes=writes)
                P.act(lambda E: E.activation(out=out2[:, 512 + c0b:1024], in_=in2[:, 512 + c0b:1024], func=func, **kw),
                      reads=reads, writes=writes)

        def s1(it):
            qb, m, p = it["qb"], it["m"], it["p"]
            if p == 0 and m == 0:
                yb = ctr["y"] % 2
                ctr["y"] += 1
                grp[qb] = yb
            it["yb"] = grp[qb]
            x = ctr["S"] % 3
            ctr["S"] += 1
            it["x"] = x
            c0a = qk("B", m, it["kba"], qb, 2 * x)
            c0b = qk("B", m, it["kbb"], qb, 2 * x + 1)
            it["c0a"], it["c0b"] = c0a, c0b
            ei = ctr["e"] % 2
            ctr["e"] += 1
            it["ei"] = ei
            act_pair(AF.Exp, e2[ei], ps2[x], c0a, c0b, [PS[2 * x], PS[2 * x + 1]], [f"e{ei}"])

        def s2(it):
            ei = it["ei"]
            act_pair(AF.Ln, sp2[ei], e2[ei], it["c0a"], it["c0b"], [f"e{ei}"], [f"sp{ei}"], bias=1.0, scale=1.0)

        def s3(it):
            p, np_, x, ei, c0a, c0b = it["p"], it["np"], it["x"], it["ei"], it["c0a"], it["c0b"]
            last = (p == np_ - 1)
            ba, bb = 2 * x, 2 * x + 1
            spa, spb = sp2[ei][:, c0a:512], sp2[ei][:, 512 + c0b:1024]
            si = ctr["ss"] % 2
            ctr["ss"] += 1
            ssk = [f"ss{2 * si}", f"ss{2 * si + 1}"]
            P.pe(lambda E: E.matmul(ps[ba][:, c0a:512], lhsT=negtri, rhs=spa, start=False, stop=True,
                                    skip_group_check=True),
                 reads=["cb", f"sp{ei}", PS[ba]], writes=[PS[ba]])
            P.pe(lambda E: E.matmul(ps[bb][:, c0b:512], lhsT=negtri, rhs=spb, start=False, stop=True,
                                    skip_group_check=True),
                 reads=["cb", f"sp{ei}", PS[bb]], writes=[PS[bb]])
            if p == 0:
                P.pool(lambda E: E.memset(Ra[:], 0.0), writes=["cqb0"])
                P.pool(lambda E: E.memset(Rb[:], 0.0), writes=["cqb1"])
                P.dve(lambda E: E.tensor_copy(out=ss2[si][:, c0a:512], in_=ps[ba][:, c0a:512]), reads=[PS[ba]],
                      writes=[ssk[0]])
            else:
                P.dve(lambda E: E.tensor_tensor(out=ss2[si][:, c0a:512], in0=ps[ba][:, c0a:512], in1=Rb[:, c0a:512],
                                                op=ALU.add), reads=[PS[ba], "cqb1"], writes=[ssk[0]])
            P.pe(lambda E: E.matmul(ps[6][:, c0a:512], lhsT=negones, rhs=spa, start=(p == 0), stop=True,
                                    skip_group_check=True),
                 reads=["cb", f"sp{ei}"], writes=[PS[6]])
            P.dve(lambda E: E.tensor_copy(out=Ra[:, c0a:512], in_=ps[6][:, c0a:512]), reads=[PS[6]], writes=["cqb0"])
            P.dve(lambda E: E.tensor_tensor(out=ss2[si][:, 512 + c0b:1024], in0=ps[bb][:, c0b:512], in1=Ra[:, c0b:512],
                                            op=ALU.add), reads=[PS[bb], "cqb0"], writes=[ssk[1]])
            if not last:
                P.pe(lambda E: E.matmul(ps[6][:, c0b:512], lhsT=negones, rhs=spb, start=False, stop=True,
                                        skip_group_check=True),
                     reads=["cb", f"sp{ei}", PS[6]], writes=[PS[6]])
                P.dve(lambda E: E.tensor_copy(out=Rb[:, c0b:512], in_=ps[6][:, c0b:512]), reads=[PS[6]], writes=["cqb1"])
            pb = ctr["pT"] % 2
            ctr["pT"] += 1
            it["pb"] = pb
            act_pair(AF.Exp, pT2[pb], ss2[si], c0a, c0b, ssk, [f"pT{2 * pb}", f"pT{2 * pb + 1}"])

        def s4(it):
            qb, m, p, np_, pb, yb = it["qb"], it["m"], it["p"], it["np"], it["pb"], it["yb"]
            kba, kbb, c0a, c0b = it["kba"], it["kbb"], it["c0a"], it["c0b"]
            Vm = VA if m == 0 else VB
            vkey = "VA" if m == 0 else "VB"
            pk = [f"pT{2 * pb}", f"pT{2 * pb + 1}"]
            P.pe(lambda E: E.matmul(ps[7][:, c0a:512], lhsT=Vm[:, kba, :], rhs=pT2[pb][:, c0a:512],
                                    start=(m == 0 and p == 0), stop=False, skip_group_check=True),
                 reads=[f"{vkey}{kba // 4}"] + pk, writes=[PS[7]])
            P.pe(lambda E: E.matmul(ps[7][:, c0b:512], lhsT=Vm[:, kbb, :], rhs=pT2[pb][:, 512 + c0b:1024],
                                    start=False, stop=(m == 1 and p == np_ - 1), skip_group_check=True),
                 reads=[f"{vkey}{kbb // 4}"] + pk, writes=[PS[7]])
            if m == 1 and p == np_ - 1:
                qsl = slice(qb * 512, (qb + 1) * 512)
                defer(0, lambda: P.dve(lambda E: E.tensor_tensor(out=y_sb[yb][:], in0=ps[7][:], in1=GT[:, qsl],
                                                                 op=ALU.mult),
                                       reads=[PS[7], f"GT{qb}"], writes=[f"y{yb}"]))
                defer(1, lambda: out_dma(u, qb, yb))

        run_pipeline(items, [s1, s2, s3, s4], [0, 1, 2, 3])

    jc = 0
    ulist = [u for u, t in enumerate(units) if t is not None]
    for ui, u in enumerate(ulist):
        utype = units[u]
        allk = [f"VA{i}" for i in range(NTB)]
        allkb = [f"VB{i}" for i in range(NTB)]
        if utype == "B" and not zeros_done[0]:
            zeros_done[0] = True
            P.pool(lambda E: E.memset(VA[:, :, 64:128], 0.0), writes=allk)
            P.pool(lambda E: E.memset(VB[:, :, 0:64], 0.0), writes=allkb)
        if utype == "C" and not ones_done[0]:
            ones_done[0] = True
            P.pool(lambda E: E.memset(VA[:, :, 64:128], 1.0), writes=allk)
            P.pool(lambda E: E.memset(VB[:, :, 64:128], 1.0), writes=allkb)
        projections(utype)
        if ui + 1 == len(ulist) and after_last_proj is not None:
            after_last_proj()
        if ui + 1 < len(ulist):
            load_weights(ulist[ui + 1], units[ulist[ui + 1]])
        if utype == "A":
            attn_A(u)
        elif utype == "B":
            attn_B(u)
        else:
            attn_C(u, jc)
            jc += 1
        unit_done(u)


KO = 12


def emit_outproj(P, B, S, ydst, xsrc, wo_d, gp_d, ostore, phase="all"):
    NTT = S // 128
    ar = B.arena
    wo_b = ar[:, 0:12288].rearrange("p (k d) -> p k d", k=KO)
    wo_f = ar[:, 12288:16384].bitcast(F32).rearrange("p (a d) -> p a d", a=2)
    gpost = ar[:, 16384:18432].bitcast(F32)
    yt = [ar[:, 18432 + i * 3072:18432 + (i + 1) * 3072].rearrange("p (k t) -> p k t", k=KO) for i in range(2)]
    xt = [ar[:, 24576 + i * 2048:24576 + (i + 1) * 2048].bitcast(F32) for i in range(2)]
    ot = [ar[:, 28672 + i * 2048:28672 + (i + 1) * 2048].bitcast(F32) for i in range(2)]
    st = B.st
    pso = B.ps2
    NTB = S // 512
    AR = ["arena"]
    if phase in ("all", "prefetch"):
        P.pool(lambda E: E.memset(st[:, 7:8], 0.0), writes=[f"hT{i}" for i in range(NTB)] + ["arena"])
        P.dma("pool", lambda E: E.dma_start(out=gpost, in_=gp_d), reads=AR, writes=["gpost"])
        for kc in range(KO):
            b = kc % 2
            P.dma("pool", lambda E, kc=kc, b=b: E.dma_start(out=wo_f[:, b, :], in_=wo_d[kc]), reads=AR,
                  writes=[f"wo_f{b}"])
            P.pool(lambda E, kc=kc, b=b: E.tensor_copy(out=wo_b[:, kc, :], in_=wo_f[:, b, :]), reads=[f"wo_f{b}"] + AR,
                   writes=["wo_b"])
        if phase == "prefetch":
            return
    for c in range(S // 256):
        yb = c % 2
        for u in range(6):
            P.dma("sp", lambda E, u=u, c=c, yb=yb: E.dma_start(
                out=yt[yb][:, 2 * u:2 * u + 2, :],
                in_=ydst[u].rearrange("(r p) t -> p r t", p=128)[:, :, c * 256:(c + 1) * 256]),
                reads=[f"ydst{u}"] + AR, writes=[f"yt{yb}"])
        for t2 in range(2):
            tt = c * 2 + t2
            b = tt % 2
            pk = [f"ps{2 * b}", f"ps{2 * b + 1}"]
            xap, xkeys = xsrc(tt)
            P.dma("sp", lambda E, xap=xap, b=b: E.dma_start(out=xt[b], in_=xap), reads=xkeys + AR, writes=[f"xt{b}"])
            for half in range(2):
                for kc in range(KO):
                    P.pe(lambda E, b=b, half=half, kc=kc, yb=yb, t2=t2: E.matmul(
                        pso[b][:, half * 512:(half + 1) * 512], lhsT=yt[yb][:, kc, t2 * 128:(t2 + 1) * 128],
                        rhs=wo_b[:, kc, half * 512:(half + 1) * 512], start=(kc == 0), stop=(kc == KO - 1)),
                        reads=[f"yt{yb}", "wo_b"] + AR, writes=[pk[half]])
            c0 = 3 * b
            sk = f"st{b}"
            P.act(lambda E, b=b, c0=c0: E.activation(out=ot[b], in_=pso[b][:], func=AF.Square,
                                                     accum_out=st[:, c0:c0 + 1]),
                  reads=pk + AR, writes=[f"ot{b}", sk])
            P.dve(lambda E, c0=c0: E.tensor_scalar(out=st[:, c0 + 1:c0 + 2], in0=st[:, c0:c0 + 1], scalar1=1.0 / D_MODEL,
                                                   scalar2=EPS, op0=ALU.mult, op1=ALU.add), reads=[sk], writes=[sk])
            P.act(lambda E, c0=c0: E.activation(out=st[:, c0 + 1:c0 + 2], in_=st[:, c0 + 1:c0 + 2], func=AF.Ln),
                  reads=[sk], writes=[sk])
            P.act(lambda E, c0=c0: E.activation(out=st[:, c0 + 2:c0 + 3], in_=st[:, c0 + 1:c0 + 2], func=AF.Exp,
                                                scale=-0.5), reads=[sk], writes=[sk])
            P.dve(lambda E, b=b, c0=c0: E.scalar_tensor_tensor(out=ot[b], in0=pso[b][:], scalar=st[:, c0 + 2:c0 + 3],
                                                               in1=gpost, op0=ALU.mult, op1=ALU.mult),
                  reads=pk + [sk, "gpost"] + AR, writes=[f"ot{b}"])
            P.dve(lambda E, b=b: E.tensor_tensor(out=ot[b], in0=ot[b], in1=xt[b], op=ALU.add),
                  reads=[f"ot{b}", f"xt{b}"] + AR, writes=[f"ot{b}"])
            oap, okeys = ostore(tt)
            P.dma("pool", lambda E, oap=oap, b=b: E.dma_start(out=oap, in_=ot[b]), reads=[f"ot{b}"] + AR, writes=okeys)


PAIRS = [[0, 1], [2, 3], [4, 5], [6, 7]]
UNITS = ("A", "A", "B", "B", "C", "C")


def build_fused(S, depth=DEPTH):
    nc = bass.Bass("TRN2", target_bir_lowering=False)
    x_d = nc.dram_tensor("x", [S, D_MODEL], F32, kind="ExternalInput").ap()
    wu_d = nc.dram_tensor("wu", [depth, 6, D_MODEL, 512], F32, kind="ExternalInput").ap()
    wff_d = nc.dram_tensor("wff", [depth, 128, KC, 4], F32, kind="ExternalInput").ap()
    gpre_d = nc.dram_tensor("gpre", [depth, 128, KC], F32, kind="ExternalInput").ap()
    fb_d = nc.dram_tensor("fb", [depth, 4, 1], F32, kind="ExternalInput").ap()
    lam_d = nc.dram_tensor("lam", [depth, 128, 256], F32, kind="ExternalInput").ap()
    subg_d = nc.dram_tensor("subg", [depth, 128, 1], F32, kind="ExternalInput").ap()
    wo_d = nc.dram_tensor("wo", [depth, KO, 128, D_MODEL], F32, kind="ExternalInput").ap()
    gp_d = nc.dram_tensor("gpost", [depth, 128, D_MODEL], F32, kind="ExternalInput").ap()
    cb_d = nc.dram_tensor("cb", [128, NCB], BF16, kind="ExternalInput").ap()
    cf_d = nc.dram_tensor("cf", [128, NCF], F32, kind="ExternalInput").ap()
    rope_d = nc.dram_tensor("rope", [2, 128, S], F32, kind="ExternalInput").ap()
    o_d = nc.dram_tensor("xo", [S, D_MODEL], F32, kind="ExternalOutput").ap()
    ysrc = [[nc.dram_tensor(f"ysrc{l}_{u}", [128, S], BF16) for u in range(6)] for l in range(depth)]
    ydst = [[nc.dram_tensor(f"ydst{l}_{u}", [256, S], BF16) for u in range(6)] for l in range(depth)]
    xmid = [nc.dram_tensor(f"xmid{l}", [S, D_MODEL], F32) for l in range(depth - 1)]
    NTB = S // 512
    with contextlib.ExitStack() as es:
        B = alloc_bufs(nc, es, S)
        P = Prog()
        emit_consts(P, B, cb_d, cf_d)
        for l in range(depth):
            lam_init = 0.8 - 0.6 * math.exp(-0.3 * l)
            if l == 0:
                xv = x_d.rearrange("(t p) d -> t p d", p=128)

                def xsrc(tt, xv=xv):
                    return xv[tt], []
            else:
                xv = xmid[l - 1].ap().rearrange("(t p) d -> t p d", p=128)

                def xsrc(tt, xv=xv, l=l):
                    return xv[tt], [f"xmid{l - 1}_{tt}"]
            prm = dict(wu=wu_d[l], wff=wff_d[l], gpre=gpre_d[l], fb=fb_d[l], lam=lam_d[l], subg=subg_d[l])

            def ystore(u, qb, l=l):
                return ysrc[l][u].ap()[:, qb * 512:(qb + 1) * 512], [f"ysrc{u}_{qb}"]

            def unit_done(u, l=l):
                P.cc(lambda E: E.collective_compute("AllGather", ALU.bypass, replica_groups=PAIRS,
                                                    ins=[ysrc[l][u].ap().opt()], outs=[ydst[l][u].ap().opt()]),
                     reads=[f"ysrc{u}_{qb}" for qb in range(NTB)], writes=[f"ydst{u}"])
            def pre_out(l=l):
                emit_outproj(P, B, S, None, None, wo_d[l], gp_d[l], None, phase="prefetch")
            emit_attn(P, B, S, lam_init, UNITS, xsrc, prm, rope_d, ystore, unit_done, after_last_proj=pre_out)
            if l < depth - 1:
                ov = xmid[l].ap().rearrange("(t p) d -> t p d", p=128)

                def ostore(tt, ov=ov, l=l):
                    return ov[tt], [f"xmid{l}_{tt}"]
            else:
                ov = o_d.rearrange("(t p) d -> t p d", p=128)

                def ostore(tt, ov=ov):
                    return ov[tt], []
            emit_outproj(P, B, S, [t.ap() for t in ydst[l]], xsrc, wo_d[l], gp_d[l], ostore, phase="main")
        P.emit(nc)
        nc._prog_stats = P.stats
    return nc


def _unit_cols(hh):
    cols = []
    for t in range(3):
        base = t * 2048
        for uu in range(2):
            j = 2 * hh + uu
            cols.append([base + g * 512 + j * 128 for g in range(4)])
    ycols = []
    for t in range(3):
        for uu in range(2):
            j = 2 * hh + uu
            ycols.append(t * 512 + j * 128)
    return cols, ycols


def _attn_inputs(x_b, w_in_l, fb_l, lam_l, subg_l, gpre_l, hh, consts):
    cb, cf, rope = consts
    cols, _ = _unit_cols(hh)
    wu = np.stack([np.concatenate([w_in_l[:, c:c + 128] for c in cc], axis=1) for cc in cols], axis=0)
    ffc = 6144 + 4 * hh
    wff = np.ascontiguousarray(w_in_l[:, ffc:ffc + 4].reshape(KC, 128, 4).transpose(1, 0, 2))
    return {
        "x": np.ascontiguousarray(x_b), "wu": np.ascontiguousarray(wu), "wff": wff,
        "gpre": np.ascontiguousarray(gpre_l.reshape(KC, 128).T),
        "fb": np.ascontiguousarray(fb_l[4 * hh:4 * hh + 4].reshape(4, 1)),
        "lam": np.ascontiguousarray(np.broadcast_to(lam_l.reshape(1, 256), (128, 256))),
        "subg": np.ascontiguousarray(subg_l.reshape(128, 1)),
        "cb": cb, "cf": cf, "rope": rope,
    }


def _wo_perm():
    rows = []
    for u in range(6):
        for r in range(2):
            _, ycols = _unit_cols(r)
            rows.append(ycols[u])
    return rows


def _fused_inputs(x_b, w_in, forget_bias, diff_lambda, diff_subln, w_out, pre_norm, post_norm, hh, consts):
    cb, cf, rope = consts
    depth = w_in.shape[0]
    per = [_attn_inputs(x_b, w_in[l], forget_bias[l], diff_lambda[l], diff_subln[l], pre_norm[l], hh, consts)
           for l in range(depth)]
    rows = _wo_perm()
    wo = np.stack([np.stack([w_out[l][r:r + 128] for r in rows], axis=0) for l in range(depth)], axis=0)
    gp = np.stack([np.broadcast_to(post_norm[l].reshape(1, D_MODEL), (128, D_MODEL)) for l in range(depth)], axis=0)
    m = {"x": np.ascontiguousarray(x_b), "cb": cb, "cf": cf, "rope": rope,
         "wo": np.ascontiguousarray(wo, dtype=np.float32), "gpost": np.ascontiguousarray(gp, dtype=np.float32)}
    for k in ("wu", "wff", "gpre", "fb", "lam", "subg"):
        m[k] = np.ascontiguousarray(np.stack([p[k] for p in per], axis=0))
    return m


def kernel(x, w_in, forget_bias, diff_lambda, diff_subln, w_out, pre_norm, post_norm):
    x = np.asarray(x, np.float32)
    w_in, forget_bias, diff_lambda, diff_subln, w_out, pre_norm, post_norm = [
        np.asarray(a, np.float32) for a in (w_in, forget_bias, diff_lambda, diff_subln, w_out, pre_norm, post_norm)]
    Bsz, S, D = x.shape
    consts = _consts(S)
    nc = build_fused(S)
    in_maps = [_fused_inputs(x[c // 2], w_in, forget_bias, diff_lambda, diff_subln, w_out, pre_norm, post_norm, c % 2,
                             consts) for c in range(8)]
    res = run_bass_kernel_spmd(nc, in_maps, core_ids=list(range(8)))
    out = np.empty_like(x)
    half = S // 2
    for c in range(8):
        b, hh = c // 2, c % 2
        out[b, hh * half:(hh + 1) * half] = res.results[c]["xo"][hh * half:(hh + 1) * half]
    return out
```
